# Optimizing a Trainium2 kernel written in Bass

```python
import jax, jax.numpy as jnp
from jax import lax
import numpy as np

D_MODEL = 4096
BATCH = 2
SEQ = 4096
DEPTH = 2
DEC_BATCH = 8
DEC_SEQ = 32
PAST_LEN = 4096

CHUNK = 64
MIX_WIDTH = D_MODEL
W_A = MIX_WIDTH // 2
W_B = MIX_WIDTH // 4
W_C = MIX_WIDTH // 4
DV_A = 128
H_A = W_A // DV_A
DK_A = DV_A // 2
QK_A = H_A * DK_A
GLA_RANK = 16
GLA_TAU = 16.0
GMLP_CHUNK = 128
H_B = 8
DH_B = W_B // H_B
H_C = 8
DH_C = W_C // H_C
CONV_W = 4
LRU_C = 8.0
D_FF = 4 * D_MODEL
EPS = 1e-6
PROJ_SIZES = (QK_A, QK_A, W_A, W_A, GLA_RANK, W_B, W_B, W_C, W_C)
PROJ_WIDTH = sum(PROJ_SIZES)

kernel_name = 'hybrid_stream_gla_gmlp_rglru'


def rmsnorm(x, g):
    xf = x.astype(jnp.float32)
    y = xf * lax.rsqrt(jnp.mean(jnp.square(xf), axis=-1, keepdims=True) + EPS)
    return (y * g.astype(jnp.float32)).astype(x.dtype)


def gla_chunked(q, k, v, log_a, s0):
    bsz, t, h, dk = q.shape
    c = min(CHUNK, t)
    n = t // c
    def to_chunks(z):
        return jnp.swapaxes(z.astype(jnp.float32).reshape(bsz, n, c, *z.shape[2:]), 0, 1)
    qs, ks, vs, gs = to_chunks(q), to_chunks(k), to_chunks(v), to_chunks(log_a)
    mask = jnp.tril(jnp.ones((c, c), bool))
    def step(s, inp):
        qc, kc, vc, gc = inp
        b = jnp.cumsum(gc, axis=1)
        b_end = b[:, -1]
        q_t = qc * jnp.exp(b)
        k_t = kc * jnp.exp(-b)
        k_e = kc * jnp.exp(b_end[:, None] - b)
        att = jnp.where(mask, jnp.einsum('bihk,bjhk->bhij', q_t, k_t), 0.0)
        o = jnp.einsum('bhij,bjhv->bihv', att, vc) + jnp.einsum('bihk,bhkv->bihv', q_t, s)
        s = jnp.exp(b_end)[..., None] * s + jnp.einsum('bjhk,bjhv->bhkv', k_e, vc)
        return s, o
    s, o = lax.scan(step, s0.astype(jnp.float32), (qs, ks, vs, gs))
    return jnp.swapaxes(o, 0, 1).reshape(bsz, t, h, -1), s


def spatial_gate(u, v, w_sp, b_sp):
    bsz, t, _ = v.shape
    c = min(GMLP_CHUNK, t)
    n = t // c
    vh = v.reshape(bsz, n, c, H_B, DH_B)
    w = jnp.where(jnp.tril(jnp.ones((c, c), bool)), w_sp[:, :c, :c], 0.0)
    mixed = jnp.einsum('hij,bnjhd->bnihd', w, vh) + b_sp[:, :c].T[None, None, :, :, None]
    return u * mixed.reshape(bsz, t, W_B).astype(u.dtype)


def causal_conv(x, buf, w, b):
    t = x.shape[1]
    xp = jnp.concatenate([buf.astype(x.dtype), x], axis=1)
    y = b + xp[:, 0:t] * w[0]
    for j in range(1, CONV_W):
        y = y + xp[:, j:j + t] * w[j]
    return y, xp[:, -(CONV_W - 1):]


def rg_lru(xc, h0, w_r, b_r, w_i, b_i, lam):
    f32 = jnp.float32
    xf = xc.astype(f32)
    bsz, t, _ = xf.shape
    xh = xf.reshape(bsz, t, H_C, DH_C)
    r = jax.nn.sigmoid(jnp.einsum('bthi,hij->bthj', xh, w_r.astype(f32)).reshape(bsz, t, W_C) + b_r.astype(f32))
    ig = jax.nn.sigmoid(jnp.einsum('bthi,hij->bthj', xh, w_i.astype(f32)).reshape(bsz, t, W_C) + b_i.astype(f32))
    log_a = -LRU_C * r * jax.nn.softplus(-lam.astype(f32))
    a = jnp.exp(log_a)
    bt = jnp.sqrt(-jnp.expm1(2.0 * log_a)) * (ig * xf)
    bt = bt.at[:, 0].add(a[:, 0] * h0.astype(f32))
    def combine(e1, e2):
        a1, b1 = e1
        a2, b2 = e2
        return a1 * a2, a2 * b1 + b2
    _, h = lax.associative_scan(combine, (a, bt), axis=1)
    return h, h[:, -1]


def trunk_layer(x, s_gla, s_conv, s_lru, ln1, w_in, w_alpha2, b_alpha, g_onorm, w_spatial, b_spatial,
                conv_w, conv_b, w_rgate, b_rgate, w_igate, b_igate, lru_lambda, w_out, ln2, w_up, w_down):
    bsz, t, _ = x.shape
    hproj = rmsnorm(x, ln1) @ w_in
    splits = np.cumsum(PROJ_SIZES)[:-1].tolist()
    q, k, v, g, a_lr, u_b, v_b, x_c, gate_c = jnp.split(hproj, splits, axis=-1)
    q = q.reshape(bsz, t, H_A, DK_A) * (DK_A ** -0.5)
    k = k.reshape(bsz, t, H_A, DK_A)
    v = v.reshape(bsz, t, H_A, DV_A)
    log_alpha = jax.nn.log_sigmoid((a_lr @ w_alpha2 + b_alpha).astype(jnp.float32)) / GLA_TAU
    o_a, s_gla_new = gla_chunked(q, k, v, log_alpha.reshape(bsz, t, H_A, DK_A), s_gla)
    o_a = rmsnorm(o_a, g_onorm).reshape(bsz, t, W_A).astype(x.dtype) * jax.nn.silu(g)
    u_b = jax.nn.gelu(u_b)
    v_b = jax.nn.gelu(v_b)
    o_b = spatial_gate(u_b, v_b, w_spatial, b_spatial)
    xcv, s_conv_new = causal_conv(x_c, s_conv, conv_w, conv_b)
    hseq, s_lru_new = rg_lru(xcv, s_lru, w_rgate, b_rgate, w_igate, b_igate, lru_lambda)
    o_c = hseq.astype(x.dtype) * jax.nn.gelu(gate_c)
    x = x + jnp.concatenate([o_a, o_b, o_c], axis=-1) @ w_out
    x = x + jnp.square(jax.nn.relu(rmsnorm(x, ln2) @ w_up)) @ w_down
    return x, s_gla_new, s_conv_new, s_lru_new, v_b


def setup_inputs(seed: int = 0) -> dict:
    key = jax.random.key(seed)
    ks = jax.random.split(key, 24)
    f32 = jnp.float32
    nrm = lambda kk, shape: jax.random.normal(kk, shape, f32)
    d = DEPTH
    a0 = jax.random.uniform(ks[18], (d, W_C), f32, 0.9, 0.999)
    a_root = a0 ** (1.0 / LRU_C)
    return {
        'x_prompt': nrm(ks[0], (BATCH, SEQ, D_MODEL)),
        'x_sample': nrm(ks[1], (DEC_BATCH, DEC_SEQ, D_MODEL)),
        'state_gla': 0.5 * nrm(ks[2], (d, DEC_BATCH, H_A, DK_A, DV_A)),
        'state_conv': nrm(ks[3], (d, DEC_BATCH, CONV_W - 1, W_C)),
        'state_lru': 0.5 * nrm(ks[4], (d, DEC_BATCH, W_C)),
        'ln1': 1.0 + 0.02 * nrm(ks[5], (d, D_MODEL)),
        'w_in': nrm(ks[6], (d, D_MODEL, PROJ_WIDTH)) * D_MODEL ** -0.5,
        'w_alpha2': nrm(ks[7], (d, GLA_RANK, QK_A)) * GLA_RANK ** -0.5,
        'b_alpha': 0.1 * nrm(ks[8], (d, QK_A)),
        'g_onorm': 1.0 + 0.02 * nrm(ks[9], (d, DV_A)),
        'w_spatial': nrm(ks[10], (d, H_B, GMLP_CHUNK, GMLP_CHUNK)) * GMLP_CHUNK ** -0.5,
        'b_spatial': 1.0 + 0.02 * nrm(ks[11], (d, H_B, GMLP_CHUNK)),
        'conv_w': nrm(ks[12], (d, CONV_W, W_C)) * CONV_W ** -0.5,
        'conv_b': 0.02 * nrm(ks[13], (d, W_C)),
        'w_rgate': nrm(ks[14], (d, H_C, DH_C, DH_C)) * DH_C ** -0.5,
        'b_rgate': 0.02 * nrm(ks[15], (d, W_C)),
        'w_igate': nrm(ks[16], (d, H_C, DH_C, DH_C)) * DH_C ** -0.5,
        'b_igate': 0.02 * nrm(ks[17], (d, W_C)),
        'lru_lambda': jnp.log(a_root) - jnp.log1p(-a_root),
        'w_out': nrm(ks[19], (d, MIX_WIDTH, D_MODEL)) * MIX_WIDTH ** -0.5,
        'ln2': 1.0 + 0.02 * nrm(ks[20], (d, D_MODEL)),
        'w_up': nrm(ks[21], (d, D_MODEL, D_FF)) * D_MODEL ** -0.5,
        'w_down': nrm(ks[22], (d, D_FF, D_MODEL)) * D_FF ** -0.5,
        'ln_final': 1.0 + 0.02 * nrm(ks[23], (D_MODEL,)),
    }


def reference(x_prompt, x_sample, state_gla, state_conv, state_lru, ln1, w_in, w_alpha2, b_alpha, g_onorm,
              w_spatial, b_spatial, conv_w, conv_b, w_rgate, b_rgate, w_igate, b_igate, lru_lambda, w_out,
              ln2, w_up, w_down, ln_final):
    bp = x_prompt.shape[0]
    zero_gla = jnp.zeros((bp, H_A, DK_A, DV_A), jnp.float32)
    zero_conv = jnp.zeros((bp, CONV_W - 1, W_C), x_prompt.dtype)
    zero_lru = jnp.zeros((bp, W_C), jnp.float32)
    xp, xs = x_prompt, x_sample
    gla_p, conv_p, lru_p, gla_s, conv_s, lru_s, v_s = [], [], [], [], [], [], []
    for l in range(DEPTH):
        w = (ln1[l], w_in[l], w_alpha2[l], b_alpha[l], g_onorm[l], w_spatial[l], b_spatial[l], conv_w[l],
             conv_b[l], w_rgate[l], b_rgate[l], w_igate[l], b_igate[l], lru_lambda[l], w_out[l], ln2[l],
             w_up[l], w_down[l])
        xp, g1, c1, h1, _ = trunk_layer(xp, zero_gla, zero_conv, zero_lru, *w)
        xs, g2, c2, h2, v2 = trunk_layer(xs, state_gla[l], state_conv[l], state_lru[l], *w)
        gla_p.append(g1); conv_p.append(c1); lru_p.append(h1)
        gla_s.append(g2); conv_s.append(c2); lru_s.append(h2); v_s.append(v2)
    y_prompt = rmsnorm(xp, ln_final)
    y_sample = rmsnorm(xs, ln_final)
    return (y_prompt, y_sample, jnp.stack(gla_p), jnp.stack(conv_p), jnp.stack(lru_p),
            jnp.stack(gla_s), jnp.stack(conv_s), jnp.stack(lru_s), jnp.stack(v_s))
```

```python
import contextlib
import numpy as np
import concourse.bass as bass
import concourse.mybir as mybir
from concourse.bass_utils import run_bass_kernel_spmd

F32 = mybir.dt.float32
BF16 = mybir.dt.bfloat16
AF = mybir.ActivationFunctionType
ALU = mybir.AluOpType

D = 4096
NCH = 32
PW = 10256
DFF = 16384
Q0, K0, V0, G0, A0, U0, VB0, XC0, GC0 = 0, 1024, 2048, 4096, 6144, 6160, 7184, 8208, 9232
EPS = 1e-6
TP = 512
PROMPT_CORES = (0, 4)
N_CORES = 8


class B:
    def __init__(self, nc, es):
        self.nc = nc
        self.es = es
        self.E = {'pe': nc.tensor, 'act': nc.scalar, 'dve': nc.vector, 'pool': nc.gpsimd, 'sp': nc.sync}
        self.sem = {}
        self.cnt = {}
        for e in ('pe', 'act', 'dve'):
            self.sem[e] = es.enter_context(nc.semaphore('s_' + e))
            self.cnt[e] = 0
        self.known = {e: {} for e in self.E}
        self.res = {}
        self.dsem = {}
        self.dcnt = {}
        self.alias = {}

    def _canon(self, keys):
        return [self.alias.get(k, k) for k in keys]

    def _deps(self, reads, writes):
        reads, writes = self._canon(reads), self._canon(writes)
        deps = {}

        def add(k, v):
            if deps.get(k, 0) < v:
                deps[k] = v
        for r in reads:
            st = self.res.get(r)
            if st and st[0]:
                add(*st[0])
        for w in writes:
            st = self.res.get(w)
            if st:
                if st[0]:
                    add(*st[0])
                for k, v in st[1].items():
                    add(k, v)
        return deps

    def _semobj(self, k):
        return self.sem[k] if k in self.sem else self.dsem[k]

    def _wait(self, eng, deps):
        kn = self.known[eng]
        for k, v in deps.items():
            if eng == 'pe' and k == 'pe':
                continue
            if kn.get(k, 0) >= v:
                continue
            self.E[eng].wait_ge(self._semobj(k), v)
            kn[k] = v

    def _mark(self, tok, reads, writes):
        reads, writes = self._canon(reads), self._canon(writes)
        k, v = tok
        for r in reads:
            st = self.res.setdefault(r, [None, {}])
            if st[1].get(k, 0) < v:
                st[1][k] = v
        for w in writes:
            self.res[w] = [tok, {}]

    def op(self, eng, fn, reads=(), writes=()):
        self._wait(eng, self._deps(reads, writes))
        inst = fn()
        self.cnt[eng] += 1
        inst.then_inc(self.sem[eng], 1)
        self._mark((eng, self.cnt[eng]), reads, writes)

    def pe_group(self, fns, reads=(), writes=()):
        self._wait('pe', self._deps(reads, writes))
        inst = None
        for fn in fns:
            inst = fn()
        self.cnt['pe'] += 1
        inst.then_inc(self.sem['pe'], 1)
        self._mark(('pe', self.cnt['pe']), reads, writes)

    def dma(self, q, out, in_, reads, writes, stream, slow=False):
        stream = self.alias.get(stream, stream)
        self._wait(q, self._deps(reads, writes))
        if stream not in self.dsem:
            self.dsem[stream] = self.es.enter_context(self.nc.semaphore('d_%d' % len(self.dsem)))
            self.dcnt[stream] = 0
        inst = self.E[q].dma_start(out=out, in_=in_, allow_slow_non_contiguous=True) if slow else self.E[q].dma_start(out=out, in_=in_)
        self.dcnt[stream] += 16
        inst.then_inc(self.dsem[stream], 16)
        self._mark((stream, self.dcnt[stream]), reads, writes)

    def finish(self):
        for k, v in self.dcnt.items():
            if v and self.known['sp'].get(k, 0) < v:
                self.E['sp'].wait_ge(self.dsem[k], v)
                self.known['sp'][k] = v


def build_program(npass, nsp=1):
    nc = bass.Bass("TRN2", target_bir_lowering=False)

    def din(name, shape):
        return nc.dram_tensor(name, shape, F32, kind="ExternalInput").ap()

    def dout(name, shape):
        return nc.dram_tensor(name, shape, F32, kind="ExternalOutput").ap()

    xp = din("xp", [npass * TP, D])
    xs = din("xs", [nsp, 32, D])
    sgla = din("sgla", [nsp, 2, 16, 64, 128])
    sconv = din("sconv", [nsp, 2, 3, 1024])
    slru = din("slru", [nsp, 2, 1024])
    ln1 = din("ln1", [2, D])
    w_in = din("w_in", [2, D, PW])
    w_a2 = din("w_alpha2", [2, 16, 1024])
    b_al = din("b_alpha", [2, 1024])
    g_on = din("g_onorm", [2, 128])
    w_sp = din("w_spatial", [2, 8, 128, 128])
    b_sp = din("b_spatial", [2, 8, 128])
    cw = din("conv_w", [2, 4, 1024])
    cb = din("conv_b", [2, 1024])
    w_r = din("w_rgate", [2, 8, 128, 128])
    b_r = din("b_rgate", [2, 1024])
    w_i = din("w_igate", [2, 8, 128, 128])
    b_i = din("b_igate", [2, 1024])
    lam = din("lru_lambda", [2, 1024])
    w_out = din("w_out", [2, D, D])
    ln2 = din("ln2", [2, D])
    w_up = din("w_up", [2, D, DFF])
    w_down = din("w_down", [2, DFF, D])
    lnf = din("ln_final", [D])
    cmask_d = din("cmask", [128, 512])
    mu_d = din("mu", [128, 128])
    masks_d = din("masks", [128, 256])
    rm_d = din("rowmask", [128, 2])
    ident_d = din("ident", [128, 128])

    yp = dout("yp", [npass * TP, D])
    ys = dout("ys", [nsp, 32, D])
    glap = dout("glap", [2, 16, 64, 128])
    convp = dout("convp", [2, 3, 1024])
    lrup = dout("lrup", [2, 1024])
    glas = dout("glas", [nsp, 2, 16, 64, 128])
    convs = dout("convs", [nsp, 2, 3, 1024])
    lrus = dout("lrus", [nsp, 2, 1024])
    vs = dout("vs", [nsp, 2, 32, 1024])
    NBLK = 249
    WCT = [nc.dram_tensor("wcache%d" % i, [100, 128, NCH * 256], BF16, kind="Internal").ap() for i in range(5)]

    class _WC:
        def __getitem__(self, key):
            l, idx = key[0], key[1]
            g = l * NBLK + idx
            return WCT[g // 100][(g % 100,) + tuple(key[2:])]
    WC = _WC()

    es = contextlib.ExitStack()
    with es:
        b = B(nc, es)

        def sb(name, shape, dt):
            return es.enter_context(nc.sbuf_tensor(name, shape, dt))

        X = sb("X", [128, NCH, TP], F32)
        XN = sb("XN", [128, NCH, TP], BF16)
        O = sb("O", [128, NCH, TP], BF16)
        STG = XN.bitcast(F32)
        WB = [sb("WB%d" % i, [128, NCH, 256], BF16) for i in range(2)]
        LN1T = sb("LN1T", [128, 2, NCH], F32)
        LN2T = sb("LN2T", [128, 2, NCH], F32)
        LNFT = sb("LNFT", [128, NCH], F32)
        NBAL = sb("NBAL", [128, 2, 8], F32)
        GON = sb("GON", [128, 2], F32)
        CWT = sb("CWT", [128, 2, 4, 8], F32)
        CBT = sb("CBT", [128, 2, 8], F32)
        BRT = sb("BRT", [128, 2, 8], F32)
        BIT = sb("BIT", [128, 2, 8], F32)
        LCT = sb("LCT", [128, 2, 8], F32)
        ONE = sb("ONE", [128, 1], F32)
        CM = sb("CM", [128, 512], BF16)
        MU = sb("MU", [128, 128], F32)
        MASKS = sb("MASKS", [128, 256], F32)
        RM = sb("RM", [128, 2], F32)
        IDENT = sb("IDENT", [128, 128], F32)
        ONES16 = sb("ONES16", [128, 128], BF16)
        WST = sb("WST", [128, 2, 8, 128], BF16)
        BSP2 = sb("BSP2", [33, 2, 8, 128], BF16)
        WA2 = sb("WA2", [16, 1024], BF16)
        WG = [sb("WG%d" % i, [128, 2, 128], BF16) for i in range(2)]
        HS = sb("HS", [128, 2, 2, 8], F32)
        CS = sb("CS", [128, 2, 2, 8, 3], F32)
        ALR = sb("ALR", [16, TP], BF16)
        LNB = sb("LNB", [128, TP], F32)
        BL = sb("BL", [128, TP], F32)
        EBa = sb("EB0", [128, TP], F32)
        E1 = sb("E1", [128, TP], F32)
        QT = sb("QT", [128, TP], BF16)
        KZ = sb("KZ", [128, 2, TP], BF16)
        KE = sb("KE", [128, TP], F32)
        VBF = KE[:, :].rearrange("p (a d) -> p a d", a=4)
        KET = sb("KET", [128, 4, 128], BF16)
        VT = sb("VT", [128, 4, 256], BF16)
        SG = sb("SG", [128, 2, TP], BF16)
        S32 = sb("S32", [128, 256], F32)
        S16 = sb("S16", [128, 256], BF16)
        TMPS = sb("TMPS", [128, 256], F32)
        ATT = [sb("ATT%d" % i, [128, 128], BF16) for i in range(2)]
        SQ = [sb("SQ%d" % i, [128, TP], BF16) for i in range(2)]
        RS = sb("RS", [128, TP], F32)
        T1a = sb("T1a", [128, TP + 8], F32)

        EB = [EBa, KE]
        T1 = [T1a, RS]
        XC, XCV, XCV16 = T1a, E1, QT
        b.alias = {('EB', 1): 'KE', ('T1', 1): 'RS', 'XC': ('T1', 0), 'XCV': 'E1', 'XCV16': 'QT', 'VBF': 'KE'}
        PS = [es.enter_context(nc.psum_tensor("PS%d" % i, [128, 512], F32)) for i in range(8)]
        rot = {'mm': [0, [0, 1, 2, 3]], 'o': [0, [4, 5]], 'sm': [0, [6, 7]]}

        def bank(kind):
            r = rot[kind]
            i = r[1][r[0] % len(r[1])]
            r[0] += 1
            return i

        def psk(i):
            return ('ps', i)

        cres = []

        def cload(dst_ap, src_ap, key):
            b.dma('sp', dst_ap, src_ap, [], [key], 'const', slow=True)
            cres.append(key)

        for l in range(2):
            cload(LN1T[:, l, :], ln1[l].rearrange("(c p) -> p c", p=128), 'LN1T')
            cload(LN2T[:, l, :], ln2[l].rearrange("(c p) -> p c", p=128), 'LN2T')
            cload(NBAL[:, l, :], b_al[l].rearrange("(c p) -> p c", p=128), 'NBAL')
            cload(GON[:, l:l + 1], g_on[l].rearrange("(p o) -> p o", o=1), 'GON')
            for j in range(4):
                cload(CWT[:, l, j, :], cw[l, j].rearrange("(c p) -> p c", p=128), 'CWT')
            cload(CBT[:, l, :], cb[l].rearrange("(c p) -> p c", p=128), 'CBT')
            cload(BRT[:, l, :], b_r[l].rearrange("(c p) -> p c", p=128), 'BRT')
            cload(BIT[:, l, :], b_i[l].rearrange("(c p) -> p c", p=128), 'BIT')
            cload(LCT[:, l, :], lam[l].rearrange("(c p) -> p c", p=128), 'LCT')
        cload(LNFT[:, :], lnf.rearrange("(c p) -> p c", p=128), 'LNFT')
        b.dma('pool', CM[:, :], cmask_d[:, :], [], ['CM'], 'CMld')
        cload(MU[:, :], mu_d[:, :], 'MU')
        cload(MASKS[:, :], masks_d[:, :], 'MASKS')
        cload(RM[:, :], rm_d[:, :], 'RM')
        cload(IDENT[:, :], ident_d[:, :], 'IDENT')
        for key in set(cres):
            b.res[key] = [('const', b.dcnt['const']), {}]

        b.op('dve', lambda: nc.vector.memset(ONE[:], 1.0), [], ['ONE'])
        b.op('dve', lambda: nc.vector.memset(ONES16[:], 1.0), [], ['ONES16'])
        b.op('dve', lambda: nc.vector.memset(BSP2[:].rearrange("p a h i -> p (a h i)"), 0.0), [], ['BSP2'])
        b.op('dve', lambda: nc.vector.memset(HS[:].rearrange("p a l c -> p (a l c)"), 0.0), [], ['HS'])
        b.op('dve', lambda: nc.vector.memset(CS[:].rearrange("p a l c j -> p (a l c j)"), 0.0), [], ['CS'])
        b.op('dve', lambda: nc.vector.tensor_scalar(NBAL[:].rearrange("p l c -> p (l c)"), NBAL[:].rearrange("p l c -> p (l c)"),
                                                    -1.0, None, ALU.mult), ['NBAL'], ['NBAL'])
        lct2 = LCT[:].rearrange("p l c -> p (l c)")
        b.op('act', lambda: nc.scalar.activation(out=lct2, in_=lct2, func=AF.Exp, scale=-1.0), ['LCT'], ['LCT'])
        b.op('act', lambda: nc.scalar.activation(out=lct2, in_=lct2, func=AF.Ln, bias=ONE[:, 0:1]), ['LCT', 'ONE'], ['LCT'])
        b.op('dve', lambda: nc.vector.tensor_scalar(lct2, lct2, -8.0, None, ALU.mult), ['LCT'], ['LCT'])
        bsp2f = BSP2[:].rearrange("p a h i -> p (a h i)")
        bspflat = b_sp.rearrange("l h i -> (l h i)").rearrange("(o n) -> o n", o=1)
        for q in range(4):
            qs = slice(q * 512, (q + 1) * 512)
            b.dma('sp', EB[0][0:1, :], bspflat[:, qs], [], [('EB', 0)], ('EB', 0))
            b.dma('sp', EB[0][32:33, :], bspflat[:, qs], [], [('EB', 0)], ('EB', 0))
            b.op('dve', lambda qs=qs: nc.vector.tensor_copy(bsp2f[0:1, qs], EB[0][0:1, :]), [('EB', 0)], ['BSP2'])
            b.op('dve', lambda qs=qs: nc.vector.tensor_copy(bsp2f[32:33, qs], EB[0][32:33, :]), [('EB', 0)], ['BSP2'])
            b.op('dve', lambda qs=qs: nc.vector.tensor_copy(EB[1][32:33, :], bsp2f[32:33, qs]), ['BSP2'], [('EB', 1)])
            b.op('dve', lambda: nc.vector.tensor_tensor(EB[1][32:33, :], EB[0][32:33, :], EB[1][32:33, :], ALU.subtract),
                 [('EB', 0), ('EB', 1)], [('EB', 1)])
            b.op('dve', lambda qs=qs: nc.vector.tensor_copy(bsp2f[32:33, qs], EB[1][32:33, :]), [('EB', 1)], ['BSP2'])
        for l in range(2):
            for h in range(8):
                stg = EB[(l * 8 + h) % 2]
                k = ('EB', (l * 8 + h) % 2)
                b.dma('sp', stg[:, 0:128], w_sp[l, h], [], [k], k)
                pi = bank('sm')
                b.pe_group([lambda pi=pi, stg=stg: nc.tensor.transpose(out=PS[pi][:, 0:128], in_=stg[:, 0:128], identity=IDENT[:, :])],
                           [k, 'IDENT'], [psk(pi)])
                b.op('dve', lambda pi=pi, l=l, h=h: nc.vector.tensor_tensor(WST[:, l, h, :], PS[pi][:, 0:128], MU[:, :], ALU.mult),
                     [psk(pi), 'MU'], ['WST'])

        wrot = [0]
        wstate = {'layer': 0, 'idx': 0, 'cached': False}

        def load_w(pieces, nk=NCH):
            s = wrot[0] % 2
            wrot[0] += 1
            l, idx = wstate['layer'], wstate['idx']
            wstate['idx'] += 1
            img = WB[s][:, 0:nk, :].rearrange("p c n -> p (c n)")
            if wstate['cached']:
                b.dma('pool', img, WC[l, idx, :, 0:nk * 256], [('wc', l, idx)], [('wb', s)], ('wb', s))
            else:
                for src, off in pieces:
                    n = src.shape[1]
                    b.dma('pool', WB[s][:, 0:nk, off:off + n], src.rearrange("(c p) n -> p c n", p=128),
                          [], [('wb', s)], ('wb', s))
                b.dma('sp', WC[l, idx, :, 0:nk * 256], img, [('wb', s)], [('wc', l, idx)], ('wbst', s))
            return s

        def proj_fm(s, off, ncols, T, rhs_buf, rhs_keys, nk=NCH):
            pi = bank('mm')
            fns = [(lambda kc=kc: nc.tensor.matmul(PS[pi][0:ncols, 0:T], WB[s][:, kc, off:off + ncols], rhs_buf[:, kc, 0:T],
                                                   start=(kc == 0), stop=(kc == nk - 1))) for kc in range(nk)]
            b.pe_group(fns, [('wb', s)] + list(rhs_keys), [psk(pi)])
            return pi

        def proj_tm(s, off, ncols, tt, tp, pi, po):
            fns = [(lambda kc=kc: nc.tensor.matmul(PS[pi][0:tp, po:po + ncols], XN[:, kc, tt * 128:tt * 128 + tp],
                                                   WB[s][:, kc, off:off + ncols], start=(kc == 0), stop=(kc == NCH - 1)))
                   for kc in range(NCH)]
            b.pe_group(fns, [('wb', s), 'XN'], [psk(pi)])

        def rmsnorm(lnT, lnkey, T):
            pi = bank('sm')
            for c in range(NCH):
                sq = SQ[c % 2]
                b.op('act', lambda c=c, sq=sq: nc.scalar.activation(out=sq[:, 0:T], in_=X[:, c, 0:T], func=AF.Square),
                     [('X', c)], [('SQ', c % 2)])
                b.op('pe', lambda c=c, sq=sq: nc.tensor.matmul(PS[pi][:, 0:T], ONES16[:, :], sq[:, 0:T], start=(c == 0), stop=(c == NCH - 1)),
                     [('SQ', c % 2), 'ONES16'], [psk(pi)])
            b.op('dve', lambda: nc.vector.tensor_scalar(RS[:, 0:T], PS[pi][:, 0:T], 1.0 / D, EPS, ALU.mult, ALU.add), [psk(pi)], ['RS'])
            b.op('act', lambda: nc.scalar.activation(out=RS[:, 0:T], in_=RS[:, 0:T], func=AF.Sqrt), ['RS'], ['RS'])
            b.op('dve', lambda: nc.vector.reciprocal(RS[:, 0:T], RS[:, 0:T]), ['RS'], ['RS'])
            for c in range(NCH):
                b.op('dve', lambda c=c: nc.vector.scalar_tensor_tensor(XN[:, c, 0:T], X[:, c, 0:T], lnT[:, c:c + 1], RS[:, 0:T],
                                                                       ALU.mult, ALU.mult),
                     [('X', c), 'RS', lnkey], ['XN'])

        def load_x(src, T):
            tp = min(T, 128)
            ntt = (T + 127) // 128
            for tt in range(ntt):
                stg = STG[:, (tt % 2) * 16:(tt % 2) * 16 + 16, :].rearrange("p a b -> p (a b)")
                k = ('stg', tt % 2)
                b.dma('sp', stg[0:tp, :], src[tt * 128:tt * 128 + tp, :], [], [k, 'XN'], k)
                for c0 in range(0, NCH, 4):
                    pi = bank('sm')
                    fns = [(lambda j=j: nc.tensor.transpose(out=PS[pi][:, j * 128:j * 128 + tp], in_=stg[0:tp, (c0 + j) * 128:(c0 + j + 1) * 128],
                                                            identity=IDENT[0:tp, 0:tp])) for j in range(4)]
                    b.pe_group(fns, [k, 'IDENT'], [psk(pi)])
                    src_v = PS[pi][:, :].rearrange("p (a t) -> p a t", a=4)[:, :, 0:tp]
                    eng = 'act' if (c0 // 4) % 2 else 'dve'
                    if eng == 'act':
                        b.op('act', lambda c0=c0, src_v=src_v: nc.scalar.copy(out=X[:, c0:c0 + 4, tt * 128:tt * 128 + tp], in_=src_v),
                             [psk(pi)], [('X', c0 + j) for j in range(4)])
                    else:
                        b.op('dve', lambda c0=c0, src_v=src_v: nc.vector.tensor_copy(X[:, c0:c0 + 4, tt * 128:tt * 128 + tp], src_v),
                             [psk(pi)], [('X', c0 + j) for j in range(4)])

        def store_y(dst, T):
            tp = min(T, 128)
            ntt = (T + 127) // 128
            pi = bank('sm')
            for c in range(NCH):
                sq = SQ[c % 2]
                b.op('act', lambda c=c, sq=sq: nc.scalar.activation(out=sq[:, 0:T], in_=X[:, c, 0:T], func=AF.Square),
                     [('X', c)], [('SQ', c % 2)])
                b.op('pe', lambda c=c, sq=sq: nc.tensor.matmul(PS[pi][:, 0:T], ONES16[:, :], sq[:, 0:T], start=(c == 0), stop=(c == NCH - 1)),
                     [('SQ', c % 2), 'ONES16'], [psk(pi)])
            b.op('dve', lambda: nc.vector.tensor_scalar(RS[:, 0:T], PS[pi][:, 0:T], 1.0 / D, EPS, ALU.mult, ALU.add), [psk(pi)], ['RS'])
            b.op('act', lambda: nc.scalar.activation(out=RS[:, 0:T], in_=RS[:, 0:T], func=AF.Sqrt), ['RS'], ['RS'])
            b.op('dve', lambda: nc.vector.reciprocal(RS[:, 0:T], RS[:, 0:T]), ['RS'], ['RS'])
            for c in range(NCH):
                b.op('dve', lambda c=c: nc.vector.scalar_tensor_tensor(X[:, c, 0:T], X[:, c, 0:T], LNFT[:, c:c + 1], RS[:, 0:T],
                                                                       ALU.mult, ALU.mult),
                     [('X', c), 'RS', 'LNFT'], [('X', c)])
            for tt in range(ntt):
                stg = STG[:, (tt % 2) * 16:(tt % 2) * 16 + 16, :].rearrange("p a b -> p (a b)")
                k = ('stg', tt % 2)
                for c0 in range(0, NCH, 4):
                    pi = bank('sm')
                    fns = [(lambda j=j: nc.tensor.transpose(out=PS[pi][0:tp, j * 128:(j + 1) * 128], in_=X[:, c0 + j, tt * 128:tt * 128 + tp],
                                                            identity=IDENT[:, :])) for j in range(4)]
                    b.pe_group(fns, [('X', c0 + j) for j in range(4)] + ['IDENT'], [psk(pi)])
                    if (c0 // 4) % 2:
                        b.op('act', lambda c0=c0, pi=pi: nc.scalar.copy(out=stg[0:tp, c0 * 128:(c0 + 4) * 128], in_=PS[pi][0:tp, :]),
                             [psk(pi)], [k, 'XN'])
                    else:
                        b.op('dve', lambda c0=c0, pi=pi: nc.vector.tensor_copy(stg[0:tp, c0 * 128:(c0 + 4) * 128], PS[pi][0:tp, :]),
                             [psk(pi)], [k, 'XN'])
                b.dma('sp', dst[tt * 128:tt * 128 + tp, :], stg[0:tp, :], [k], [], k)

        def layer(l, T, sq_, first, last, gla_in, gla_out, conv_in, conv_out, lru_in, lru_out, vs_out):
            tp = min(T, 128)
            ntt = (T + 127) // 128
            CH = tp
            nchk = T // CH
            wstate['layer'] = l
            wstate['idx'] = 0
            rmsnorm(LN1T[:, l, :], 'LN1T', T)
            s = load_w([(w_in[l][:, A0:A0 + 16], 0)])
            pi = proj_fm(s, 0, 16, T, XN, ['XN'])
            b.op('act', lambda pi=pi: nc.scalar.copy(out=ALR[0:16, 0:T], in_=PS[pi][0:16, 0:T]), [psk(pi)], ['ALR'])
            b.dma('pool', WA2[:, :], w_a2[l], [], ['WA2'], 'WA2')
            def stage1(m):
                sA = load_w([(w_in[l][:, Q0 + m * 128:Q0 + (m + 1) * 128], 0), (w_in[l][:, K0 + m * 128:K0 + (m + 1) * 128], 128)])
                return proj_fm(sA, 0, 128, T, XN, ['XN']), proj_fm(sA, 128, 128, T, XN, ['XN'])

            nxt = stage1(0)
            for m in range(8):
                pq, pk = nxt
                pa = bank('sm')
                b.pe_group([lambda pa=pa, m=m: nc.tensor.matmul(PS[pa][:, 0:T], WA2[0:16, m * 128:(m + 1) * 128], ALR[0:16, 0:T], start=True, stop=True)],
                           ['WA2', 'ALR'], [psk(pa)])
                b.op('act', lambda pa=pa, m=m: nc.scalar.activation(out=LNB[:, 0:T], in_=PS[pa][:, 0:T], func=AF.Exp, scale=-1.0, bias=NBAL[:, l, m:m + 1]),
                     [psk(pa), 'NBAL'], ['LNB'])
                b.op('act', lambda: nc.scalar.activation(out=LNB[:, 0:T], in_=LNB[:, 0:T], func=AF.Ln, bias=ONE[:, 0:1]), ['LNB', 'ONE'], ['LNB'])
                b.op('dve', lambda: nc.vector.tensor_tensor_scan(BL[:, 0:T], CM[:, 0:T], LNB[:, 0:T], 0.0, ALU.mult, ALU.add),
                     ['LNB', 'CM'], ['BL'])
                b.op('act', lambda: nc.scalar.activation(out=E1[:, 0:T], in_=BL[:, 0:T], func=AF.Exp, scale=-1.0 / 16), ['BL'], ['E1'])
                b.op('act', lambda: nc.scalar.activation(out=EB[0][:, 0:T], in_=BL[:, 0:T], func=AF.Exp, scale=1.0 / 16), ['BL'], [('EB', 0)])
                blend = BL[:, CH - 1:T:CH].rearrange("p (c o) -> p c o", o=1).broadcast_to([128, nchk, CH])
                b.op('dve', lambda blend=blend: nc.vector.tensor_tensor(EB[1][:, 0:T].rearrange("p (c t) -> p c t", t=CH), blend,
                                                                       BL[:, 0:T].rearrange("p (c t) -> p c t", t=CH), ALU.subtract),
                     ['BL'], [('EB', 1)])
                b.op('act', lambda: nc.scalar.activation(out=EB[1][:, 0:T], in_=EB[1][:, 0:T], func=AF.Exp, scale=-1.0 / 16), [('EB', 1)], [('EB', 1)])
                b.op('dve', lambda pq=pq: nc.vector.scalar_tensor_tensor(QT[:, 0:T], PS[pq][:, 0:T], 0.125, E1[:, 0:T], ALU.mult, ALU.mult),
                     [psk(pq), 'E1'], ['QT'])
                for hh in range(2):
                    b.op('dve', lambda pk=pk, hh=hh: nc.vector.scalar_tensor_tensor(KZ[:, hh, 0:T], PS[pk][:, 0:T], RM[:, hh:hh + 1], EB[0][:, 0:T],
                                                                                   ALU.mult, ALU.mult),
                         [psk(pk), ('EB', 0), 'RM'], [('KZ', hh)])
                b.op('dve', lambda pk=pk: nc.vector.tensor_tensor(KE[:, 0:T], PS[pk][:, 0:T], EB[1][:, 0:T], ALU.mult), [psk(pk), ('EB', 1)], ['KE'])
                pt = bank('sm')
                b.pe_group([(lambda tt=tt: nc.tensor.transpose(out=PS[pt][0:tp, tt * 128:(tt + 1) * 128], in_=KE[:, tt * 128:tt * 128 + tp],
                                                               identity=IDENT[:, :])) for tt in range(ntt)], ['KE', 'IDENT'], [psk(pt)])
                b.op('act', lambda pt=pt: nc.scalar.copy(out=KET[0:tp, 0:ntt, :].rearrange("p a d -> p (a d)"), in_=PS[pt][0:tp, 0:ntt * 128]),
                     [psk(pt)], ['KET'])
                sB = load_w([(w_in[l][:, V0 + m * 256:V0 + (m + 1) * 256], 0)])
                for t2 in range(0, ntt, 2):
                    pv = bank('mm')
                    n2 = min(2, ntt - t2)
                    for j in range(n2):
                        proj_tm(sB, 0, 256, t2 + j, tp, pv, j * 256)
                    b.op('act', lambda pv=pv, t2=t2, n2=n2: nc.scalar.copy(out=VT[0:tp, t2:t2 + n2, :].rearrange("p a d -> p (a d)"),
                                                                         in_=PS[pv][0:tp, 0:n2 * 256]), [psk(pv)], ['VT'])
                sC = load_w([(w_in[l][:, G0 + m * 256:G0 + (m + 1) * 256], 0)])
                for hh in range(2):
                    pg = proj_fm(sC, hh * 128, 128, T, XN, ['XN'])
                    b.op('act', lambda pg=pg, hh=hh: nc.scalar.activation(out=SG[:, hh, 0:T], in_=PS[pg][:, 0:T], func=AF.Silu),
                         [psk(pg)], [('SG', hh)])
                if gla_in is None:
                    b.op('dve', lambda: nc.vector.memset(S32[:, :], 0.0), [], ['S32'])
                else:
                    if first:
                        b.op('dve', lambda: nc.vector.memset(S32[:, :], 0.0), [], ['S32'])
                    for hh in range(2):
                        b.dma('sp', S32[hh * 64:(hh + 1) * 64, hh * 128:(hh + 1) * 128], gla_in[l, 2 * m + hh],
                              [('glad', id(gla_in), l, 2 * m + hh)], ['S32'], 'S32')
                b.op('act', lambda: nc.scalar.copy(out=S16[:, :], in_=S32[:, :]), ['S32'], ['S16'])
                po = [bank('o'), bank('o')]
                for c in range(nchk):
                    cs = slice(c * CH, (c + 1) * CH)
                    for hh in range(2):
                        pa = bank('sm')
                        b.pe_group([lambda pa=pa, hh=hh, cs=cs: nc.tensor.matmul(PS[pa][0:CH, 0:CH], KZ[:, hh, cs], QT[:, cs], start=True, stop=True)],
                                   [('KZ', hh), 'QT'], [psk(pa)])
                        b.op('dve', lambda pa=pa, hh=hh: nc.vector.tensor_tensor(ATT[hh][0:CH, 0:CH], PS[pa][0:CH, 0:CH], MU[0:CH, 0:CH], ALU.mult),
                             [psk(pa), 'MU'], [('ATT', hh)])
                        b.pe_group([lambda hh=hh, c=c, cs=cs: nc.tensor.matmul(PS[po[hh]][:, cs], VT[0:CH, c, hh * 128:(hh + 1) * 128], ATT[hh][0:CH, 0:CH],
                                                                               start=True, stop=False),
                                    lambda hh=hh, cs=cs: nc.tensor.matmul(PS[po[hh]][:, cs], S16[:, hh * 128:(hh + 1) * 128], QT[:, cs],
                                                                          start=False, stop=True)],
                                   ['VT', ('ATT', hh), 'S16', 'QT'], [psk(po[hh])])
                    pd = bank('sm')
                    b.pe_group([lambda pd=pd, c=c: nc.tensor.matmul(PS[pd][:, 0:256], KET[0:CH, c, :], VT[0:CH, c, :], start=True, stop=True)],
                               ['KET', 'VT'], [psk(pd)])
                    b.op('dve', lambda pd=pd: nc.vector.tensor_tensor(TMPS[:, :], PS[pd][:, 0:256], MASKS[:, :], ALU.mult), [psk(pd), 'MASKS'], ['TMPS'])
                    b.op('dve', lambda c=c: nc.vector.scalar_tensor_tensor(S32[:, :], S32[:, :], E1[:, (c + 1) * CH - 1:(c + 1) * CH], TMPS[:, :],
                                                                          ALU.mult, ALU.add), ['S32', 'E1', 'TMPS'], ['S32'])
                    if c < nchk - 1:
                        b.op('act', lambda: nc.scalar.copy(out=S16[:, :], in_=S32[:, :]), ['S32'], ['S16'])
                    if c == 0 and m + 1 < 8:
                        nxt = stage1(m + 1)
                for hh in range(2):
                    b.dma('sp', gla_out[l, 2 * m + hh], S32[hh * 64:(hh + 1) * 64, hh * 128:(hh + 1) * 128],
                          ['S32'], [('glad', id(gla_out), l, 2 * m + hh)], 'S32st')
                for hh in range(2):
                    sq = SQ[hh]
                    b.op('act', lambda hh=hh, sq=sq: nc.scalar.activation(out=sq[:, 0:T], in_=PS[po[hh]][:, 0:T], func=AF.Square),
                         [psk(po[hh])], [('SQ', hh)])
                    pn = bank('sm')
                    b.pe_group([lambda pn=pn, sq=sq: nc.tensor.matmul(PS[pn][:, 0:T], ONES16[:, :], sq[:, 0:T], start=True, stop=True)],
                               [('SQ', hh), 'ONES16'], [psk(pn)])
                    b.op('dve', lambda pn=pn: nc.vector.tensor_scalar(RS[:, 0:T], PS[pn][:, 0:T], 1.0 / 128, EPS, ALU.mult, ALU.add), [psk(pn)], ['RS'])
                    b.op('act', lambda: nc.scalar.activation(out=RS[:, 0:T], in_=RS[:, 0:T], func=AF.Sqrt), ['RS'], ['RS'])
                    b.op('dve', lambda: nc.vector.reciprocal(RS[:, 0:T], RS[:, 0:T]), ['RS'], ['RS'])
                    b.op('dve', lambda hh=hh: nc.vector.tensor_tensor(T1[hh][:, 0:T], PS[po[hh]][:, 0:T], RS[:, 0:T], ALU.mult),
                         [psk(po[hh]), 'RS'], [('T1', hh)])
                    b.op('dve', lambda hh=hh, m=m: nc.vector.scalar_tensor_tensor(O[:, 2 * m + hh, 0:T], T1[hh][:, 0:T], GON[:, l:l + 1], SG[:, hh, 0:T],
                                                                                ALU.mult, ALU.mult),
                         [('T1', hh), 'GON', ('SG', hh)], [('O', 2 * m + hh)])
            def gstage(gb):
                s = load_w([(w_in[l][:, U0 + gb * 128:U0 + (gb + 1) * 128], 0), (w_in[l][:, VB0 + gb * 128:VB0 + (gb + 1) * 128], 128)])
                pu = proj_fm(s, 0, 128, T, XN, ['XN'])
                pv = bank('mm')
                for tt in range(ntt):
                    proj_tm(s, 128, 128, tt, tp, pv, tt * 128)
                return pu, pv

            gnx = gstage(0)
            for gb in range(8):
                pu, pv = gnx
                b.op('act', lambda pu=pu: nc.scalar.activation(out=T1[0][:, 0:T], in_=PS[pu][:, 0:T], func=AF.Gelu), [psk(pu)], [('T1', 0)])
                b.op('act', lambda pv=pv: nc.scalar.activation(out=VBF[0:tp, 0:ntt, :].rearrange("p a d -> p (a d)"), in_=PS[pv][0:tp, 0:ntt * 128],
                                                              func=AF.Gelu), [psk(pv)], ['VBF'])
                b.op('dve', lambda: nc.vector.tensor_copy(KET[0:tp, 0:ntt, :].rearrange("p a d -> p (a d)"),
                                                          VBF[0:tp, 0:ntt, :].rearrange("p a d -> p (a d)")), ['VBF'], ['KET'])
                if vs_out is not None:
                    b.dma('sp', vs_out[l, :, gb * 128:(gb + 1) * 128], VBF[0:tp, 0, :], ['VBF'], [], 'VBF')
                if gb + 1 < 8:
                    gnx = gstage(gb + 1)
                pm = bank('mm')
                for tt in range(ntt):
                    b.pe_group([lambda tt=tt, pm=pm, gb=gb: nc.tensor.matmul(PS[pm][:, tt * 128:tt * 128 + tp], KET[0:tp, tt, :], WST[0:tp, l, gb, 0:tp],
                                                                           start=True, stop=False),
                                lambda tt=tt, pm=pm, gb=gb: nc.tensor.matmul(PS[pm][:, tt * 128:tt * 128 + tp], ONES16[0:33, :], BSP2[0:33, l, gb, 0:tp],
                                                                           start=False, stop=True)],
                               ['KET', 'WST', 'ONES16', 'BSP2'], [psk(pm)])
                b.op('dve', lambda pm=pm, gb=gb: nc.vector.tensor_tensor(O[:, 16 + gb, 0:T], PS[pm][:, 0:T], T1[0][:, 0:T], ALU.mult),
                     [psk(pm), ('T1', 0)], [('O', 16 + gb)])
            def lstage(cbk):
                s = load_w([(w_in[l][:, XC0 + cbk * 128:XC0 + (cbk + 1) * 128], 0), (w_in[l][:, GC0 + cbk * 128:GC0 + (cbk + 1) * 128], 128)])
                return proj_fm(s, 0, 128, T, XN, ['XN']), proj_fm(s, 128, 128, T, XN, ['XN'])

            lnx = lstage(0)
            for cbk in range(8):
                px, pc = lnx
                wg = WG[cbk % 2]
                kwg = ('WG', cbk % 2)
                b.dma('pool', wg[:, 0, :], w_r[l, cbk], [], [kwg], kwg)
                b.dma('pool', wg[:, 1, :], w_i[l, cbk], [], [kwg], kwg)
                if conv_in is None:
                    if first:
                        b.op('dve', lambda: nc.vector.memset(XC[:, 0:3], 0.0), [], ['XC'])
                    else:
                        b.op('dve', lambda cbk=cbk: nc.vector.tensor_copy(XC[:, 0:3], CS[:, sq_, l, cbk, :]), ['CS'], ['XC'])
                else:
                    for j3 in range(3):
                        b.dma('sp', XC[:, j3:j3 + 1], conv_in[l, j3, cbk * 128:(cbk + 1) * 128].rearrange("(p o) -> p o", o=1), [], ['XC'], 'XC')
                b.op('act', lambda px=px: nc.scalar.copy(out=XC[:, 3:3 + T], in_=PS[px][:, 0:T]), [psk(px)], ['XC'])
                b.op('dve', lambda cbk=cbk: nc.vector.tensor_copy(CS[:, sq_, l, cbk, :], XC[:, T:T + 3]), ['XC'], ['CS'])
                if last:
                    for j3 in range(3):
                        b.dma('sp', conv_out[l, j3, cbk * 128:(cbk + 1) * 128].rearrange("(p o) -> p o", o=1), XC[:, T + j3:T + j3 + 1], ['XC'], [], 'XCst')
                b.op('dve', lambda cbk=cbk: nc.vector.tensor_scalar(XCV[:, 0:T], XC[:, 0:T], CWT[:, l, 0, cbk:cbk + 1], CBT[:, l, cbk:cbk + 1],
                                                                   ALU.mult, ALU.add), ['XC', 'CWT', 'CBT'], ['XCV'])
                for j in range(1, 4):
                    b.op('dve', lambda cbk=cbk, j=j: nc.vector.scalar_tensor_tensor(XCV[:, 0:T], XC[:, j:j + T], CWT[:, l, j, cbk:cbk + 1], XCV[:, 0:T],
                                                                                   ALU.mult, ALU.add), ['XC', 'CWT', 'XCV'], ['XCV'])
                if cbk + 1 < 8:
                    lnx = lstage(cbk + 1)
                b.op('act', lambda: nc.scalar.copy(out=XCV16[:, 0:T], in_=XCV[:, 0:T]), ['XCV'], ['XCV16'])
                pr = bank('sm')
                b.pe_group([lambda pr=pr, wg=wg: nc.tensor.matmul(PS[pr][:, 0:T], wg[:, 0, :], XCV16[:, 0:T], start=True, stop=True)],
                           [kwg, 'XCV16'], [psk(pr)])
                pg = bank('sm')
                b.pe_group([lambda pg=pg, wg=wg: nc.tensor.matmul(PS[pg][:, 0:T], wg[:, 1, :], XCV16[:, 0:T], start=True, stop=True)],
                           [kwg, 'XCV16'], [psk(pg)])
                A_, Bt, Ig = EB[0], EB[1], LNB
                b.op('act', lambda pr=pr, cbk=cbk: nc.scalar.activation(out=A_[:, 0:T], in_=PS[pr][:, 0:T], func=AF.Sigmoid, bias=BRT[:, l, cbk:cbk + 1]),
                     [psk(pr), 'BRT'], [('EB', 0)])
                b.op('act', lambda pg=pg, cbk=cbk: nc.scalar.activation(out=Ig[:, 0:T], in_=PS[pg][:, 0:T], func=AF.Sigmoid, bias=BIT[:, l, cbk:cbk + 1]),
                     [psk(pg), 'BIT'], ['LNB'])
                b.op('act', lambda cbk=cbk: nc.scalar.activation(out=A_[:, 0:T], in_=A_[:, 0:T], func=AF.Exp, scale=LCT[:, l, cbk:cbk + 1]),
                     [('EB', 0), 'LCT'], [('EB', 0)])
                b.op('dve', lambda: nc.vector.tensor_tensor(Bt[:, 0:T], A_[:, 0:T], A_[:, 0:T], ALU.mult), [('EB', 0)], [('EB', 1)])
                b.op('dve', lambda: nc.vector.tensor_scalar(Bt[:, 0:T], Bt[:, 0:T], -1.0, 1.0, ALU.mult, ALU.add), [('EB', 1)], [('EB', 1)])
                b.op('act', lambda: nc.scalar.activation(out=Bt[:, 0:T], in_=Bt[:, 0:T], func=AF.Sqrt), [('EB', 1)], [('EB', 1)])
                b.op('dve', lambda: nc.vector.tensor_tensor(Ig[:, 0:T], Ig[:, 0:T], XCV[:, 0:T], ALU.mult), ['LNB', 'XCV'], ['LNB'])
                b.op('dve', lambda: nc.vector.tensor_tensor(Bt[:, 0:T], Bt[:, 0:T], Ig[:, 0:T], ALU.mult), [('EB', 1), 'LNB'], [('EB', 1)])
                if lru_in is not None:
                    b.dma('sp', HS[:, sq_, l, cbk:cbk + 1], lru_in[l, cbk * 128:(cbk + 1) * 128].rearrange("(p o) -> p o", o=1), [], ['HS'], 'HS')
                b.op('dve', lambda cbk=cbk: nc.vector.tensor_tensor_scan(BL[:, 0:T], A_[:, 0:T], Bt[:, 0:T], HS[:, sq_, l, cbk:cbk + 1], ALU.mult, ALU.add),
                     [('EB', 0), ('EB', 1), 'HS'], ['BL'])
                b.op('dve', lambda cbk=cbk: nc.vector.tensor_copy(HS[:, sq_, l, cbk:cbk + 1], BL[:, T - 1:T]), ['BL'], ['HS'])
                if last:
                    b.dma('sp', lru_out[l, cbk * 128:(cbk + 1) * 128].rearrange("(p o) -> p o", o=1), HS[:, sq_, l, cbk:cbk + 1], ['HS'], [], 'HSst')
                b.op('act', lambda pc=pc: nc.scalar.activation(out=T1[1][:, 0:T], in_=PS[pc][:, 0:T], func=AF.Gelu), [psk(pc)], [('T1', 1)])
                b.op('dve', lambda cbk=cbk: nc.vector.tensor_tensor(O[:, 24 + cbk, 0:T], BL[:, 0:T], T1[1][:, 0:T], ALU.mult),
                     ['BL', ('T1', 1)], [('O', 24 + cbk)])
            okeys = [('O', c) for c in range(NCH)]
            for oc2 in range(16):
                s = load_w([(w_out[l][:, oc2 * 256:(oc2 + 1) * 256], 0)])
                for cc in range(2):
                    pi = proj_fm(s, cc * 128, 128, T, O, okeys)
                    c = oc2 * 2 + cc
                    b.op('dve', lambda pi=pi, c=c: nc.vector.tensor_tensor(X[:, c, 0:T], X[:, c, 0:T], PS[pi][:, 0:T], ALU.add),
                         [psk(pi), ('X', c)], [('X', c)])
            rmsnorm(LN2T[:, l, :], 'LN2T', T)
            for blk in range(8):
                hb = O[:, (blk % 2) * 16:(blk % 2) * 16 + 16, :]
                hkeys = [('O', (blk % 2) * 16 + j) for j in range(16)]
                for j in range(8):
                    s = load_w([(w_up[l][:, blk * 2048 + j * 256:blk * 2048 + (j + 1) * 256], 0)])
                    for cc in range(2):
                        pi = proj_fm(s, cc * 128, 128, T, XN, ['XN'])
                        t1 = T1[cc]
                        b.op('act', lambda pi=pi, t1=t1: nc.scalar.activation(out=t1[:, 0:T], in_=PS[pi][:, 0:T], func=AF.Relu), [psk(pi)], [('T1', cc)])
                        b.op('dve', lambda t1=t1, j=j, cc=cc, hb=hb: nc.vector.tensor_tensor(hb[:, j * 2 + cc, 0:T], t1[:, 0:T], t1[:, 0:T], ALU.mult),
                             [('T1', cc)], [hkeys[j * 2 + cc]])
                for oc2 in range(16):
                    s = load_w([(w_down[l][blk * 2048:(blk + 1) * 2048, oc2 * 256:(oc2 + 1) * 256], 0)], nk=16)
                    for cc in range(2):
                        pi = proj_fm(s, cc * 128, 128, T, hb, hkeys, nk=16)
                        c = oc2 * 2 + cc
                        b.op('dve', lambda pi=pi, c=c: nc.vector.tensor_tensor(X[:, c, 0:T], X[:, c, 0:T], PS[pi][:, 0:T], ALU.add),
                             [psk(pi), ('X', c)], [('X', c)])

        for p in range(npass):
            wstate['cached'] = p > 0
            load_x(xp[p * TP:(p + 1) * TP, :], TP)
            for l in range(2):
                layer(l, TP, 0, p == 0, p == npass - 1, None if p == 0 else glap, glap, None, convp, None, lrup, None)
            store_y(yp[p * TP:(p + 1) * TP, :], TP)
        for sp_ in range(nsp):
            wstate['cached'] = npass > 0 or sp_ > 0
            load_x(xs[sp_], 32)
            for l in range(2):
                layer(l, 32, 1, True, True, sgla[sp_], glas[sp_], sconv[sp_], convs[sp_], slru[sp_], lrus[sp_], vs[sp_])
            store_y(ys[sp_], 32)
        b.finish()
    return nc


def _consts():
    t = np.arange(512)
    cmask = np.broadcast_to((t % 128 != 0).astype(np.float32)[None, :], (128, 512)).copy()
    j = np.arange(128)
    mu = (j[:, None] <= j[None, :]).astype(np.float32)
    masks = ((j[:, None] // 64) == (np.arange(256)[None, :] // 128)).astype(np.float32)
    rowmask = ((j[:, None] // 64) == np.arange(2)[None, :]).astype(np.float32)
    ident = np.eye(128, dtype=np.float32)
    return dict(cmask=cmask, mu=mu, masks=masks, rowmask=rowmask, ident=ident)


_PROG = {}

CORES = 8
PROMPT_AT = {0: 0, 4: 1}
NSP = 1


def run(inputs, npass, cores=None, prompt_at=None, nsp=None, trace=False):
    cores = CORES if cores is None else cores
    prompt_at = PROMPT_AT if prompt_at is None else prompt_at
    nsp = NSP if nsp is None else nsp
    key = (npass, nsp)
    if key not in _PROG:
        _PROG[key] = build_program(npass, nsp)
    nc = _PROG[key]
    f = lambda a: np.ascontiguousarray(np.asarray(a, dtype=np.float32))
    shared = {k: f(inputs[k]) for k in ('ln1', 'w_in', 'w_alpha2', 'b_alpha', 'g_onorm', 'w_spatial', 'b_spatial', 'conv_w', 'conv_b',
                                        'w_rgate', 'b_rgate', 'w_igate', 'b_igate', 'lru_lambda', 'w_out', 'ln2', 'w_up', 'w_down',
                                        'ln_final')}
    shared.update(_consts())
    xpf = f(inputs['x_prompt'])
    xsf = f(inputs['x_sample'])
    sg, scv, slr = f(inputs['state_gla']), f(inputs['state_conv']), f(inputs['state_lru'])
    zeros_p = np.zeros((npass * TP, D), np.float32)
    nsamp = min(xsf.shape[0], cores * nsp)
    in_maps = []
    for c in range(cores):
        m = dict(shared)
        m['xp'] = np.ascontiguousarray(xpf[prompt_at[c], :npass * TP]) if c in prompt_at else zeros_p
        ids = [min(c * nsp + j, xsf.shape[0] - 1) for j in range(nsp)]
        m['xs'] = np.ascontiguousarray(xsf[ids])
        m['sgla'] = np.ascontiguousarray(np.stack([sg[:, i] for i in ids]))
        m['sconv'] = np.ascontiguousarray(np.stack([scv[:, i] for i in ids]))
        m['slru'] = np.ascontiguousarray(np.stack([slr[:, i] for i in ids]))
        in_maps.append(m)
    res = run_bass_kernel_spmd(nc, in_maps, core_ids=list(range(cores)), **({'trace': True} if trace else {}))
    r = res.results
    pcs = sorted(prompt_at, key=lambda c: prompt_at[c])
    y_prompt = np.stack([r[c]['yp'] for c in pcs])
    gla_p = np.stack([r[c]['glap'] for c in pcs], axis=1)
    conv_p = np.stack([r[c]['convp'] for c in pcs], axis=1)
    lru_p = np.stack([r[c]['lrup'] for c in pcs], axis=1)
    samp = [(i // nsp, i % nsp) for i in range(nsamp)]
    y_sample = np.stack([r[c]['ys'][j] for c, j in samp])
    gla_s = np.stack([r[c]['glas'][j] for c, j in samp], axis=1)
    conv_s = np.stack([r[c]['convs'][j] for c, j in samp], axis=1)
    lru_s = np.stack([r[c]['lrus'][j] for c, j in samp], axis=1)
    v_s = np.stack([r[c]['vs'][j] for c, j in samp], axis=1)
    outs = (y_prompt, y_sample, gla_p, conv_p, lru_p, gla_s, conv_s, lru_s, v_s)
    return tuple(np.ascontiguousarray(o.astype(np.float32)) for o in outs), res


def kernel(**inputs):
    outs, _ = run(inputs, 4096 // TP)
    return outs
```

```python
import contextlib
import numpy as np
import concourse.bass as bass
import concourse.mybir as mybir
from concourse.bass_utils import run_bass_kernel_spmd

F32 = mybir.dt.float32
BF16 = mybir.dt.bfloat16
AF = mybir.ActivationFunctionType
ALU = mybir.AluOpType

D = 4096
NCH = 32
PW = 10256
DFF = 16384
Q0, K0, V0, G0, A0, U0, VB0, XC0, GC0 = 0, 1024, 2048, 4096, 6144, 6160, 7184, 8208, 9232
EPS = 1e-6
TP = 512
PROMPT_CORES = (0, 4)
N_CORES = 8


class B:
    def __init__(self, nc, es):
        self.nc = nc
        self.es = es
        self.E = {'pe': nc.tensor, 'act': nc.scalar, 'dve': nc.vector, 'pool': nc.gpsimd, 'sp': nc.sync}
        self.sem = {}
        self.cnt = {}
        for e in ('pe', 'act', 'dve'):
            self.sem[e] = es.enter_context(nc.semaphore('s_' + e))
            self.cnt[e] = 0
        self.known = {e: {} for e in self.E}
        self.res = {}
        self.dsem = {}
        self.dcnt = {}
        self.alias = {}

    def _canon(self, keys):
        return [self.alias.get(k, k) for k in keys]

    def _deps(self, reads, writes):
        reads, writes = self._canon(reads), self._canon(writes)
        deps = {}

        def add(k, v):
            if deps.get(k, 0) < v:
                deps[k] = v
        for r in reads:
            st = self.res.get(r)
            if st and st[0]:
                add(*st[0])
        for w in writes:
            st = self.res.get(w)
            if st:
                if st[0]:
                    add(*st[0])
                for k, v in st[1].items():
                    add(k, v)
        return deps

    def _semobj(self, k):
        return self.sem[k] if k in self.sem else self.dsem[k]

    def _wait(self, eng, deps):
        kn = self.known[eng]
        for k, v in deps.items():
            if eng == 'pe' and k == 'pe':
                continue
            if kn.get(k, 0) >= v:
                continue
            self.E[eng].wait_ge(self._semobj(k), v)
            kn[k] = v

    def _mark(self, tok, reads, writes):
        reads, writes = self._canon(reads), self._canon(writes)
        k, v = tok
        for r in reads:
            st = self.res.setdefault(r, [None, {}])
            if st[1].get(k, 0) < v:
                st[1][k] = v
        for w in writes:
            self.res[w] = [tok, {}]

    def op(self, eng, fn, reads=(), writes=()):
        self._wait(eng, self._deps(reads, writes))
        inst = fn()
        self.cnt[eng] += 1
        inst.then_inc(self.sem[eng], 1)
        self._mark((eng, self.cnt[eng]), reads, writes)

    def pe_group(self, fns, reads=(), writes=()):
        self._wait('pe', self._deps(reads, writes))
        inst = None
        for fn in fns:
            inst = fn()
        self.cnt['pe'] += 1
        inst.then_inc(self.sem['pe'], 1)
        self._mark(('pe', self.cnt['pe']), reads, writes)

    def dma(self, q, out, in_, reads, writes, stream, slow=False):
        stream = self.alias.get(stream, stream)
        self._wait(q, self._deps(reads, writes))
        if stream not in self.dsem:
            self.dsem[stream] = self.es.enter_context(self.nc.semaphore('d_%d' % len(self.dsem)))
            self.dcnt[stream] = 0
        inst = self.E[q].dma_start(out=out, in_=in_, allow_slow_non_contiguous=True) if slow else self.E[q].dma_start(out=out, in_=in_)
        self.dcnt[stream] += 16
        inst.then_inc(self.dsem[stream], 16)
        self._mark((stream, self.dcnt[stream]), reads, writes)

    def finish(self):
        for k, v in self.dcnt.items():
            if v and self.known['sp'].get(k, 0) < v:
                self.E['sp'].wait_ge(self.dsem[k], v)
                self.known['sp'][k] = v


def build_program(npass, nsp=1):
    nc = bass.Bass("TRN2", target_bir_lowering=False)

    def din(name, shape):
        return nc.dram_tensor(name, shape, F32, kind="ExternalInput").ap()

    def dout(name, shape):
        return nc.dram_tensor(name, shape, F32, kind="ExternalOutput").ap()

    xp = din("xp", [npass * TP, D])
    xs = din("xs", [nsp, 32, D])
    sgla = din("sgla", [nsp, 2, 16, 64, 128])
    sconv = din("sconv", [nsp, 2, 3, 1024])
    slru = din("slru", [nsp, 2, 1024])
    ln1 = din("ln1", [2, D])
    w_in = din("w_in", [2, D, PW])
    w_a2 = din("w_alpha2", [2, 16, 1024])
    b_al = din("b_alpha", [2, 1024])
    g_on = din("g_onorm", [2, 128])
    w_sp = din("w_spatial", [2, 8, 128, 128])
    b_sp = din("b_spatial", [2, 8, 128])
    cw = din("conv_w", [2, 4, 1024])
    cb = din("conv_b", [2, 1024])
    w_r = din("w_rgate", [2, 8, 128, 128])
    b_r = din("b_rgate", [2, 1024])
    w_i = din("w_igate", [2, 8, 128, 128])
    b_i = din("b_igate", [2, 1024])
    lam = din("lru_lambda", [2, 1024])
    w_out = din("w_out", [2, D, D])
    ln2 = din("ln2", [2, D])
    w_up = din("w_up", [2, D, DFF])
    w_down = din("w_down", [2, DFF, D])
    lnf = din("ln_final", [D])
    cmask_d = din("cmask", [128, 512])
    mu_d = din("mu", [128, 128])
    masks_d = din("masks", [128, 256])
    rm_d = din("rowmask", [128, 2])
    ident_d = din("ident", [128, 128])

    yp = dout("yp", [npass * TP, D])
    ys = dout("ys", [nsp, 32, D])
    glap = dout("glap", [2, 16, 64, 128])
    convp = dout("convp", [2, 3, 1024])
    lrup = dout("lrup", [2, 1024])
    glas = dout("glas", [nsp, 2, 16, 64, 128])
    convs = dout("convs", [nsp, 2, 3, 1024])
    lrus = dout("lrus", [nsp, 2, 1024])
    vs = dout("vs", [nsp, 2, 32, 1024])
    NBLK = 249
    WCT = [nc.dram_tensor("wcache%d" % i, [100, 128, NCH * 256], BF16, kind="Internal").ap() for i in range(5)]

    class _WC:
        def __getitem__(self, key):
            l, idx = key[0], key[1]
            g = l * NBLK + idx
            return WCT[g // 100][(g % 100,) + tuple(key[2:])]
    WC = _WC()

    es = contextlib.ExitStack()
    with es:
        b = B(nc, es)

        def sb(name, shape, dt):
            return es.enter_context(nc.sbuf_tensor(name, shape, dt))

        X = sb("X", [128, NCH, TP], F32)
        XN = sb("XN", [128, NCH, TP], BF16)
        O = sb("O", [128, NCH, TP], BF16)
        STG = XN.bitcast(F32)
        WB = [sb("WB%d" % i, [128, NCH, 256], BF16) for i in range(2)]
        LN1T = sb("LN1T", [128, 2, NCH], F32)
        LN2T = sb("LN2T", [128, 2, NCH], F32)
        LNFT = sb("LNFT", [128, NCH], F32)
        NBAL = sb("NBAL", [128, 2, 8], F32)
        GON = sb("GON", [128, 2], F32)
        CWT = sb("CWT", [128, 2, 4, 8], F32)
        CBT = sb("CBT", [128, 2, 8], F32)
        BRT = sb("BRT", [128, 2, 8], F32)
        BIT = sb("BIT", [128, 2, 8], F32)
        LCT = sb("LCT", [128, 2, 8], F32)
        ONE = sb("ONE", [128, 1], F32)
        CM = sb("CM", [128, 512], BF16)
        MU = sb("MU", [128, 128], F32)
        MASKS = sb("MASKS", [128, 256], F32)
        RM = sb("RM", [128, 2], F32)
        IDENT = sb("IDENT", [128, 128], F32)
        ONES16 = sb("ONES16", [128, 128], BF16)
        WST = sb("WST", [128, 2, 8, 128], BF16)
        BSP2 = sb("BSP2", [33, 2, 8, 128], BF16)
        WA2 = sb("WA2", [16, 1024], BF16)
        WG = [sb("WG%d" % i, [128, 2, 128], BF16) for i in range(2)]
        HS = sb("HS", [128, 2, 2, 8], F32)
        CS = sb("CS", [128, 2, 2, 8, 3], F32)
        ALR = sb("ALR", [16, TP], BF16)
        LNB = sb("LNB", [128, TP], F32)
        BL = sb("BL", [128, TP], F32)
        EBa = sb("EB0", [128, TP], F32)
        E1 = sb("E1", [128, TP], F32)
        QT = sb("QT", [128, TP], BF16)
        KZ = sb("KZ", [128, 2, TP], BF16)
        KE = sb("KE", [128, TP], F32)
        VBF = KE[:, :].rearrange("p (a d) -> p a d", a=4)
        KET = sb("KET", [128, 4, 128], BF16)
        VT = sb("VT", [128, 4, 256], BF16)
        SG = sb("SG", [128, 2, TP], BF16)
        S32 = sb("S32", [128, 256], F32)
        S16 = sb("S16", [128, 256], BF16)
        TMPS = sb("TMPS", [128, 256], F32)
        ATT = [sb("ATT%d" % i, [128, 128], BF16) for i in range(2)]
        SQ = [sb("SQ%d" % i, [128, TP], BF16) for i in range(2)]
        RS = sb("RS", [128, TP], F32)
        T1a = sb("T1a", [128, TP + 8], F32)

        EB = [EBa, KE]
        T1 = [T1a, RS]
        XC, XCV, XCV16 = T1a, E1, QT
        b.alias = {('EB', 1): 'KE', ('T1', 1): 'RS', 'XC': ('T1', 0), 'XCV': 'E1', 'XCV16': 'QT', 'VBF': 'KE'}
        PS = [es.enter_context(nc.psum_tensor("PS%d" % i, [128, 512], F32)) for i in range(8)]
        rot = {'mm': [0, [0, 1, 2, 3]], 'o': [0, [4, 5]], 'sm': [0, [6, 7]]}

        def bank(kind):
            r = rot[kind]
            i = r[1][r[0] % len(r[1])]
            r[0] += 1
            return i

        def psk(i):
            return ('ps', i)

        cres = []

        def cload(dst_ap, src_ap, key):
            b.dma('sp', dst_ap, src_ap, [], [key], 'const', slow=True)
            cres.append(key)

        for l in range(2):
            cload(LN1T[:, l, :], ln1[l].rearrange("(c p) -> p c", p=128), 'LN1T')
            cload(LN2T[:, l, :], ln2[l].rearrange("(c p) -> p c", p=128), 'LN2T')
            cload(NBAL[:, l, :], b_al[l].rearrange("(c p) -> p c", p=128), 'NBAL')
            cload(GON[:, l:l + 1], g_on[l].rearrange("(p o) -> p o", o=1), 'GON')
            for j in range(4):
                cload(CWT[:, l, j, :], cw[l, j].rearrange("(c p) -> p c", p=128), 'CWT')
            cload(CBT[:, l, :], cb[l].rearrange("(c p) -> p c", p=128), 'CBT')
            cload(BRT[:, l, :], b_r[l].rearrange("(c p) -> p c", p=128), 'BRT')
            cload(BIT[:, l, :], b_i[l].rearrange("(c p) -> p c", p=128), 'BIT')
            cload(LCT[:, l, :], lam[l].rearrange("(c p) -> p c", p=128), 'LCT')
        cload(LNFT[:, :], lnf.rearrange("(c p) -> p c", p=128), 'LNFT')
        b.dma('pool', CM[:, :], cmask_d[:, :], [], ['CM'], 'CMld')
        cload(MU[:, :], mu_d[:, :], 'MU')
        cload(MASKS[:, :], masks_d[:, :], 'MASKS')
        cload(RM[:, :], rm_d[:, :], 'RM')
        cload(IDENT[:, :], ident_d[:, :], 'IDENT')
        for key in set(cres):
            b.res[key] = [('const', b.dcnt['const']), {}]

        b.op('dve', lambda: nc.vector.memset(ONE[:], 1.0), [], ['ONE'])
        b.op('dve', lambda: nc.vector.memset(ONES16[:], 1.0), [], ['ONES16'])
        b.op('dve', lambda: nc.vector.memset(BSP2[:].rearrange("p a h i -> p (a h i)"), 0.0), [], ['BSP2'])
        b.op('dve', lambda: nc.vector.memset(HS[:].rearrange("p a l c -> p (a l c)"), 0.0), [], ['HS'])
        b.op('dve', lambda: nc.vector.memset(CS[:].rearrange("p a l c j -> p (a l c j)"), 0.0), [], ['CS'])
        b.op('dve', lambda: nc.vector.tensor_scalar(NBAL[:].rearrange("p l c -> p (l c)"), NBAL[:].rearrange("p l c -> p (l c)"),
                                                    -1.0, None, ALU.mult), ['NBAL'], ['NBAL'])
        lct2 = LCT[:].rearrange("p l c -> p (l c)")
        b.op('act', lambda: nc.scalar.activation(out=lct2, in_=lct2, func=AF.Exp, scale=-1.0), ['LCT'], ['LCT'])
        b.op('act', lambda: nc.scalar.activation(out=lct2, in_=lct2, func=AF.Ln, bias=ONE[:, 0:1]), ['LCT', 'ONE'], ['LCT'])
        b.op('dve', lambda: nc.vector.tensor_scalar(lct2, lct2, -8.0, None, ALU.mult), ['LCT'], ['LCT'])
        bsp2f = BSP2[:].rearrange("p a h i -> p (a h i)")
        bspflat = b_sp.rearrange("l h i -> (l h i)").rearrange("(o n) -> o n", o=1)
        for q in range(4):
            qs = slice(q * 512, (q + 1) * 512)
            b.dma('sp', EB[0][0:1, :], bspflat[:, qs], [], [('EB', 0)], ('EB', 0))
            b.dma('sp', EB[0][32:33, :], bspflat[:, qs], [], [('EB', 0)], ('EB', 0))
            b.op('dve', lambda qs=qs: nc.vector.tensor_copy(bsp2f[0:1, qs], EB[0][0:1, :]), [('EB', 0)], ['BSP2'])
            b.op('dve', lambda qs=qs: nc.vector.tensor_copy(bsp2f[32:33, qs], EB[0][32:33, :]), [('EB', 0)], ['BSP2'])
            b.op('dve', lambda qs=qs: nc.vector.tensor_copy(EB[1][32:33, :], bsp2f[32:33, qs]), ['BSP2'], [('EB', 1)])
            b.op('dve', lambda: nc.vector.tensor_tensor(EB[1][32:33, :], EB[0][32:33, :], EB[1][32:33, :], ALU.subtract),
                 [('EB', 0), ('EB', 1)], [('EB', 1)])
            b.op('dve', lambda qs=qs: nc.vector.tensor_copy(bsp2f[32:33, qs], EB[1][32:33, :]), [('EB', 1)], ['BSP2'])
        for l in range(2):
            for h in range(8):
                stg = EB[(l * 8 + h) % 2]
                k = ('EB', (l * 8 + h) % 2)
                b.dma('sp', stg[:, 0:128], w_sp[l, h], [], [k], k)
                pi = bank('sm')
                b.pe_group([lambda pi=pi, stg=stg: nc.tensor.transpose(out=PS[pi][:, 0:128], in_=stg[:, 0:128], identity=IDENT[:, :])],
                           [k, 'IDENT'], [psk(pi)])
                b.op('dve', lambda pi=pi, l=l, h=h: nc.vector.tensor_tensor(WST[:, l, h, :], PS[pi][:, 0:128], MU[:, :], ALU.mult),
                     [psk(pi), 'MU'], ['WST'])

        wrot = [0]
        wstate = {'layer': 0, 'idx': 0, 'cached': False}

        WBW = [w[:].rearrange("p c n -> p (c n)").rearrange("p (k n) -> p k n", k=16, n=512) for w in WB]

        def load_w(pieces, nk=NCH, wide=False):
            s = wrot[0] % 2
            wrot[0] += 1
            l, idx = wstate['layer'], wstate['idx']
            wstate['idx'] += 1
            assert idx < NBLK
            view = WBW[s] if wide else WB[s]
            nimg = 16 * 512 if wide else nk * 256
            img = WB[s][:].rearrange("p c n -> p (c n)")[:, 0:nimg]
            if wstate['cached']:
                b.dma('pool', img, WC[l, idx, :, 0:nimg], [('wc', l, idx)], [('wb', s)], ('wb', s))
            else:
                for src, off in pieces:
                    n = src.shape[1]
                    b.dma('pool', view[:, 0:nk, off:off + n], src.rearrange("(c p) n -> p c n", p=128),
                          [], [('wb', s)], ('wb', s))
                b.dma('sp', WC[l, idx, :, 0:nimg], img, [('wb', s)], [('wc', l, idx)], ('wbst', s))
            return s

        def proj_fm(s, off, ncols, T, rhs_buf, rhs_keys, nk=NCH, wide=False):
            pi = bank('mm')
            wv = WBW[s] if wide else WB[s]
            fns = [(lambda kc=kc: nc.tensor.matmul(PS[pi][0:ncols, 0:T], wv[:, kc, off:off + ncols], rhs_buf[:, kc, 0:T],
                                                   start=(kc == 0), stop=(kc == nk - 1))) for kc in range(nk)]
            b.pe_group(fns, [('wb', s)] + list(rhs_keys), [psk(pi)])
            return pi

        def proj_tm(s, off, ncols, tt, tp, pi, po):
            fns = [(lambda kc=kc: nc.tensor.matmul(PS[pi][0:tp, po:po + ncols], XN[:, kc, tt * 128:tt * 128 + tp],
                                                   WB[s][:, kc, off:off + ncols], start=(kc == 0), stop=(kc == NCH - 1)))
                   for kc in range(NCH)]
            b.pe_group(fns, [('wb', s), 'XN'], [psk(pi)])

        def rmsnorm(lnT, lnkey, T):
            pi = bank('sm')
            for c in range(NCH):
                sq = SQ[c % 2]
                b.op('act', lambda c=c, sq=sq: nc.scalar.activation(out=sq[:, 0:T], in_=X[:, c, 0:T], func=AF.Square),
                     [('X', c)], [('SQ', c % 2)])
                b.op('pe', lambda c=c, sq=sq: nc.tensor.matmul(PS[pi][:, 0:T], ONES16[:, :], sq[:, 0:T], start=(c == 0), stop=(c == NCH - 1)),
                     [('SQ', c % 2), 'ONES16'], [psk(pi)])
            b.op('dve', lambda: nc.vector.tensor_scalar(RS[:, 0:T], PS[pi][:, 0:T], 1.0 / D, EPS, ALU.mult, ALU.add), [psk(pi)], ['RS'])
            b.op('act', lambda: nc.scalar.activation(out=RS[:, 0:T], in_=RS[:, 0:T], func=AF.Sqrt), ['RS'], ['RS'])
            b.op('dve', lambda: nc.vector.reciprocal(RS[:, 0:T], RS[:, 0:T]), ['RS'], ['RS'])
            for c in range(NCH):
                b.op('dve', lambda c=c: nc.vector.scalar_tensor_tensor(XN[:, c, 0:T], X[:, c, 0:T], lnT[:, c:c + 1], RS[:, 0:T],
                                                                       ALU.mult, ALU.mult),
                     [('X', c), 'RS', lnkey], ['XN'])

        def load_x(src, T):
            tp = min(T, 128)
            ntt = (T + 127) // 128
            for tt in range(ntt):
                stg = STG[:, (tt % 2) * 16:(tt % 2) * 16 + 16, :].rearrange("p a b -> p (a b)")
                k = ('stg', tt % 2)
                b.dma('sp', stg[0:tp, :], src[tt * 128:tt * 128 + tp, :], [], [k, 'XN'], k)
                for c0 in range(0, NCH, 4):
                    pi = bank('sm')
                    fns = [(lambda j=j: nc.tensor.transpose(out=PS[pi][:, j * 128:j * 128 + tp], in_=stg[0:tp, (c0 + j) * 128:(c0 + j + 1) * 128],
                                                            identity=IDENT[0:tp, 0:tp])) for j in range(4)]
                    b.pe_group(fns, [k, 'IDENT'], [psk(pi)])
                    src_v = PS[pi][:, :].rearrange("p (a t) -> p a t", a=4)[:, :, 0:tp]
                    eng = 'act' if (c0 // 4) % 2 else 'dve'
                    if eng == 'act':
                        b.op('act', lambda c0=c0, src_v=src_v: nc.scalar.copy(out=X[:, c0:c0 + 4, tt * 128:tt * 128 + tp], in_=src_v),
                             [psk(pi)], [('X', c0 + j) for j in range(4)])
                    else:
                        b.op('dve', lambda c0=c0, src_v=src_v: nc.vector.tensor_copy(X[:, c0:c0 + 4, tt * 128:tt * 128 + tp], src_v),
                             [psk(pi)], [('X', c0 + j) for j in range(4)])

        def store_y(dst, T):
            tp = min(T, 128)
            ntt = (T + 127) // 128
            pi = bank('sm')
            for c in range(NCH):
                sq = SQ[c % 2]
                b.op('act', lambda c=c, sq=sq: nc.scalar.activation(out=sq[:, 0:T], in_=X[:, c, 0:T], func=AF.Square),
                     [('X', c)], [('SQ', c % 2)])
                b.op('pe', lambda c=c, sq=sq: nc.tensor.matmul(PS[pi][:, 0:T], ONES16[:, :], sq[:, 0:T], start=(c == 0), stop=(c == NCH - 1)),
                     [('SQ', c % 2), 'ONES16'], [psk(pi)])
            b.op('dve', lambda: nc.vector.tensor_scalar(RS[:, 0:T], PS[pi][:, 0:T], 1.0 / D, EPS, ALU.mult, ALU.add), [psk(pi)], ['RS'])
            b.op('act', lambda: nc.scalar.activation(out=RS[:, 0:T], in_=RS[:, 0:T], func=AF.Sqrt), ['RS'], ['RS'])
            b.op('dve', lambda: nc.vector.reciprocal(RS[:, 0:T], RS[:, 0:T]), ['RS'], ['RS'])
            for c in range(NCH):
                b.op('dve', lambda c=c: nc.vector.scalar_tensor_tensor(X[:, c, 0:T], X[:, c, 0:T], LNFT[:, c:c + 1], RS[:, 0:T],
                                                                       ALU.mult, ALU.mult),
                     [('X', c), 'RS', 'LNFT'], [('X', c)])
            for tt in range(ntt):
                stg = STG[:, (tt % 2) * 16:(tt % 2) * 16 + 16, :].rearrange("p a b -> p (a b)")
                k = ('stg', tt % 2)
                for c0 in range(0, NCH, 4):
                    pi = bank('sm')
                    fns = [(lambda j=j: nc.tensor.transpose(out=PS[pi][0:tp, j * 128:(j + 1) * 128], in_=X[:, c0 + j, tt * 128:tt * 128 + tp],
                                                            identity=IDENT[:, :])) for j in range(4)]
                    b.pe_group(fns, [('X', c0 + j) for j in range(4)] + ['IDENT'], [psk(pi)])
                    if (c0 // 4) % 2:
                        b.op('act', lambda c0=c0, pi=pi: nc.scalar.copy(out=stg[0:tp, c0 * 128:(c0 + 4) * 128], in_=PS[pi][0:tp, :]),
                             [psk(pi)], [k, 'XN'])
                    else:
                        b.op('dve', lambda c0=c0, pi=pi: nc.vector.tensor_copy(stg[0:tp, c0 * 128:(c0 + 4) * 128], PS[pi][0:tp, :]),
                             [psk(pi)], [k, 'XN'])
                b.dma('sp', dst[tt * 128:tt * 128 + tp, :], stg[0:tp, :], [k], [], k)

        def layer(l, T, sq_, first, last, gla_in, gla_out, conv_in, conv_out, lru_in, lru_out, vs_out):
            tp = min(T, 128)
            ntt = (T + 127) // 128
            CH = tp
            nchk = T // CH
            wstate['layer'] = l
            wstate['idx'] = 0
            rmsnorm(LN1T[:, l, :], 'LN1T', T)
            s = load_w([(w_in[l][:, A0:A0 + 16], 0)])
            pi = proj_fm(s, 0, 16, T, XN, ['XN'])
            b.op('act', lambda pi=pi: nc.scalar.copy(out=ALR[0:16, 0:T], in_=PS[pi][0:16, 0:T]), [psk(pi)], ['ALR'])
            b.dma('pool', WA2[:, :], w_a2[l], [], ['WA2'], 'WA2')
            def stage1(m):
                sA = load_w([(w_in[l][:, Q0 + m * 128:Q0 + (m + 1) * 128], 0), (w_in[l][:, K0 + m * 128:K0 + (m + 1) * 128], 128)])
                return proj_fm(sA, 0, 128, T, XN, ['XN']), proj_fm(sA, 128, 128, T, XN, ['XN'])

            nxt = stage1(0)
            for m in range(8):
                pq, pk = nxt
                pa = bank('sm')
                b.pe_group([lambda pa=pa, m=m: nc.tensor.matmul(PS[pa][:, 0:T], WA2[0:16, m * 128:(m + 1) * 128], ALR[0:16, 0:T], start=True, stop=True)],
                           ['WA2', 'ALR'], [psk(pa)])
                b.op('act', lambda pa=pa, m=m: nc.scalar.activation(out=LNB[:, 0:T], in_=PS[pa][:, 0:T], func=AF.Exp, scale=-1.0, bias=NBAL[:, l, m:m + 1]),
                     [psk(pa), 'NBAL'], ['LNB'])
                b.op('act', lambda: nc.scalar.activation(out=LNB[:, 0:T], in_=LNB[:, 0:T], func=AF.Ln, bias=ONE[:, 0:1]), ['LNB', 'ONE'], ['LNB'])
                b.op('dve', lambda: nc.vector.tensor_tensor_scan(BL[:, 0:T], CM[:, 0:T], LNB[:, 0:T], 0.0, ALU.mult, ALU.add),
                     ['LNB', 'CM'], ['BL'])
                b.op('act', lambda: nc.scalar.activation(out=E1[:, 0:T], in_=BL[:, 0:T], func=AF.Exp, scale=-1.0 / 16), ['BL'], ['E1'])
                b.op('act', lambda: nc.scalar.activation(out=EB[0][:, 0:T], in_=BL[:, 0:T], func=AF.Exp, scale=1.0 / 16), ['BL'], [('EB', 0)])
                blend = BL[:, CH - 1:T:CH].rearrange("p (c o) -> p c o", o=1).broadcast_to([128, nchk, CH])
                b.op('dve', lambda blend=blend: nc.vector.tensor_tensor(EB[1][:, 0:T].rearrange("p (c t) -> p c t", t=CH), blend,
                                                                       BL[:, 0:T].rearrange("p (c t) -> p c t", t=CH), ALU.subtract),
                     ['BL'], [('EB', 1)])
                b.op('act', lambda: nc.scalar.activation(out=EB[1][:, 0:T], in_=EB[1][:, 0:T], func=AF.Exp, scale=-1.0 / 16), [('EB', 1)], [('EB', 1)])
                b.op('dve', lambda pq=pq: nc.vector.scalar_tensor_tensor(QT[:, 0:T], PS[pq][:, 0:T], 0.125, E1[:, 0:T], ALU.mult, ALU.mult),
                     [psk(pq), 'E1'], ['QT'])
                for hh in range(2):
                    b.op('dve', lambda pk=pk, hh=hh: nc.vector.scalar_tensor_tensor(KZ[:, hh, 0:T], PS[pk][:, 0:T], RM[:, hh:hh + 1], EB[0][:, 0:T],
                                                                                   ALU.mult, ALU.mult),
                         [psk(pk), ('EB', 0), 'RM'], [('KZ', hh)])
                b.op('dve', lambda pk=pk: nc.vector.tensor_tensor(KE[:, 0:T], PS[pk][:, 0:T], EB[1][:, 0:T], ALU.mult), [psk(pk), ('EB', 1)], ['KE'])
                pt = bank('sm')
                b.pe_group([(lambda tt=tt: nc.tensor.transpose(out=PS[pt][0:tp, tt * 128:(tt + 1) * 128], in_=KE[:, tt * 128:tt * 128 + tp],
                                                               identity=IDENT[:, :])) for tt in range(ntt)], ['KE', 'IDENT'], [psk(pt)])
                b.op('act', lambda pt=pt: nc.scalar.copy(out=KET[0:tp, 0:ntt, :].rearrange("p a d -> p (a d)"), in_=PS[pt][0:tp, 0:ntt * 128]),
                     [psk(pt)], ['KET'])
                sB = load_w([(w_in[l][:, V0 + m * 256:V0 + (m + 1) * 256], 0)])
                for t2 in range(0, ntt, 2):
                    pv = bank('mm')
                    n2 = min(2, ntt - t2)
                    for j in range(n2):
                        proj_tm(sB, 0, 256, t2 + j, tp, pv, j * 256)
                    b.op('act', lambda pv=pv, t2=t2, n2=n2: nc.scalar.copy(out=VT[0:tp, t2:t2 + n2, :].rearrange("p a d -> p (a d)"),
                                                                         in_=PS[pv][0:tp, 0:n2 * 256]), [psk(pv)], ['VT'])
                sC = load_w([(w_in[l][:, G0 + m * 256:G0 + (m + 1) * 256], 0)])
                for hh in range(2):
                    pg = proj_fm(sC, hh * 128, 128, T, XN, ['XN'])
                    b.op('act', lambda pg=pg, hh=hh: nc.scalar.activation(out=SG[:, hh, 0:T], in_=PS[pg][:, 0:T], func=AF.Silu),
                         [psk(pg)], [('SG', hh)])
                if gla_in is None:
                    b.op('dve', lambda: nc.vector.memset(S32[:, :], 0.0), [], ['S32'])
                else:
                    if first:
                        b.op('dve', lambda: nc.vector.memset(S32[:, :], 0.0), [], ['S32'])
                    for hh in range(2):
                        b.dma('sp', S32[hh * 64:(hh + 1) * 64, hh * 128:(hh + 1) * 128], gla_in[l, 2 * m + hh],
                              [('glad', id(gla_in), l, 2 * m + hh)], ['S32'], 'S32')
                b.op('act', lambda: nc.scalar.copy(out=S16[:, :], in_=S32[:, :]), ['S32'], ['S16'])
                po = [bank('o'), bank('o')]
                for c in range(nchk):
                    cs = slice(c * CH, (c + 1) * CH)
                    for hh in range(2):
                        pa = bank('sm')
                        b.pe_group([lambda pa=pa, hh=hh, cs=cs: nc.tensor.matmul(PS[pa][0:CH, 0:CH], KZ[:, hh, cs], QT[:, cs], start=True, stop=True)],
                                   [('KZ', hh), 'QT'], [psk(pa)])
                        b.op('dve', lambda pa=pa, hh=hh: nc.vector.tensor_tensor(ATT[hh][0:CH, 0:CH], PS[pa][0:CH, 0:CH], MU[0:CH, 0:CH], ALU.mult),
                             [psk(pa), 'MU'], [('ATT', hh)])
                        b.pe_group([lambda hh=hh, c=c, cs=cs: nc.tensor.matmul(PS[po[hh]][:, cs], VT[0:CH, c, hh * 128:(hh + 1) * 128], ATT[hh][0:CH, 0:CH],
                                                                               start=True, stop=False),
                                    lambda hh=hh, cs=cs: nc.tensor.matmul(PS[po[hh]][:, cs], S16[:, hh * 128:(hh + 1) * 128], QT[:, cs],
                                                                          start=False, stop=True)],
                                   ['VT', ('ATT', hh), 'S16', 'QT'], [psk(po[hh])])
                    pd = bank('sm')
                    b.pe_group([lambda pd=pd, c=c: nc.tensor.matmul(PS[pd][:, 0:256], KET[0:CH, c, :], VT[0:CH, c, :], start=True, stop=True)],
                               ['KET', 'VT'], [psk(pd)])
                    b.op('dve', lambda pd=pd: nc.vector.tensor_tensor(TMPS[:, :], PS[pd][:, 0:256], MASKS[:, :], ALU.mult), [psk(pd), 'MASKS'], ['TMPS'])
                    b.op('dve', lambda c=c: nc.vector.scalar_tensor_tensor(S32[:, :], S32[:, :], E1[:, (c + 1) * CH - 1:(c + 1) * CH], TMPS[:, :],
                                                                          ALU.mult, ALU.add), ['S32', 'E1', 'TMPS'], ['S32'])
                    if c < nchk - 1:
                        b.op('act', lambda: nc.scalar.copy(out=S16[:, :], in_=S32[:, :]), ['S32'], ['S16'])
                    if c == 0 and m + 1 < 8:
                        nxt = stage1(m + 1)
                for hh in range(2):
                    b.dma('sp', gla_out[l, 2 * m + hh], S32[hh * 64:(hh + 1) * 64, hh * 128:(hh + 1) * 128],
                          ['S32'], [('glad', id(gla_out), l, 2 * m + hh)], 'S32st')
                for hh in range(2):
                    sq = SQ[hh]
                    b.op('act', lambda hh=hh, sq=sq: nc.scalar.activation(out=sq[:, 0:T], in_=PS[po[hh]][:, 0:T], func=AF.Square),
                         [psk(po[hh])], [('SQ', hh)])
                    pn = bank('sm')
                    b.pe_group([lambda pn=pn, sq=sq: nc.tensor.matmul(PS[pn][:, 0:T], ONES16[:, :], sq[:, 0:T], start=True, stop=True)],
                               [('SQ', hh), 'ONES16'], [psk(pn)])
                    b.op('dve', lambda pn=pn: nc.vector.tensor_scalar(RS[:, 0:T], PS[pn][:, 0:T], 1.0 / 128, EPS, ALU.mult, ALU.add), [psk(pn)], ['RS'])
                    b.op('act', lambda: nc.scalar.activation(out=RS[:, 0:T], in_=RS[:, 0:T], func=AF.Sqrt), ['RS'], ['RS'])
                    b.op('dve', lambda: nc.vector.reciprocal(RS[:, 0:T], RS[:, 0:T]), ['RS'], ['RS'])
                    b.op('dve', lambda hh=hh: nc.vector.tensor_tensor(T1[hh][:, 0:T], PS[po[hh]][:, 0:T], RS[:, 0:T], ALU.mult),
                         [psk(po[hh]), 'RS'], [('T1', hh)])
                    b.op('dve', lambda hh=hh, m=m: nc.vector.scalar_tensor_tensor(O[:, 2 * m + hh, 0:T], T1[hh][:, 0:T], GON[:, l:l + 1], SG[:, hh, 0:T],
                                                                                ALU.mult, ALU.mult),
                         [('T1', hh), 'GON', ('SG', hh)], [('O', 2 * m + hh)])
            def gstage(gb):
                s = load_w([(w_in[l][:, U0 + gb * 128:U0 + (gb + 1) * 128], 0), (w_in[l][:, VB0 + gb * 128:VB0 + (gb + 1) * 128], 128)])
                pu = proj_fm(s, 0, 128, T, XN, ['XN'])
                pv = bank('mm')
                for tt in range(ntt):
                    proj_tm(s, 128, 128, tt, tp, pv, tt * 128)
                return pu, pv

            gnx = gstage(0)
            for gb in range(8):
                pu, pv = gnx
                b.op('act', lambda pu=pu: nc.scalar.activation(out=T1[0][:, 0:T], in_=PS[pu][:, 0:T], func=AF.Gelu), [psk(pu)], [('T1', 0)])
                b.op('act', lambda pv=pv: nc.scalar.activation(out=VBF[0:tp, 0:ntt, :].rearrange("p a d -> p (a d)"), in_=PS[pv][0:tp, 0:ntt * 128],
                                                              func=AF.Gelu), [psk(pv)], ['VBF'])
                b.op('dve', lambda: nc.vector.tensor_copy(KET[0:tp, 0:ntt, :].rearrange("p a d -> p (a d)"),
                                                          VBF[0:tp, 0:ntt, :].rearrange("p a d -> p (a d)")), ['VBF'], ['KET'])
                if vs_out is not None:
                    b.dma('sp', vs_out[l, :, gb * 128:(gb + 1) * 128], VBF[0:tp, 0, :], ['VBF'], [], 'VBF')
                if gb + 1 < 8:
                    gnx = gstage(gb + 1)
                pm = bank('mm')
                for tt in range(ntt):
                    b.pe_group([lambda tt=tt, pm=pm, gb=gb: nc.tensor.matmul(PS[pm][:, tt * 128:tt * 128 + tp], KET[0:tp, tt, :], WST[0:tp, l, gb, 0:tp],
                                                                           start=True, stop=False),
                                lambda tt=tt, pm=pm, gb=gb: nc.tensor.matmul(PS[pm][:, tt * 128:tt * 128 + tp], ONES16[0:33, :], BSP2[0:33, l, gb, 0:tp],
                                                                           start=False, stop=True)],
                               ['KET', 'WST', 'ONES16', 'BSP2'], [psk(pm)])
                b.op('dve', lambda pm=pm, gb=gb: nc.vector.tensor_tensor(O[:, 16 + gb, 0:T], PS[pm][:, 0:T], T1[0][:, 0:T], ALU.mult),
                     [psk(pm), ('T1', 0)], [('O', 16 + gb)])
            def lstage(cbk):
                s = load_w([(w_in[l][:, XC0 + cbk * 128:XC0 + (cbk + 1) * 128], 0), (w_in[l][:, GC0 + cbk * 128:GC0 + (cbk + 1) * 128], 128)])
                return proj_fm(s, 0, 128, T, XN, ['XN']), proj_fm(s, 128, 128, T, XN, ['XN'])

            lnx = lstage(0)
            for cbk in range(8):
                px, pc = lnx
                wg = WG[cbk % 2]
                kwg = ('WG', cbk % 2)
                b.dma('pool', wg[:, 0, :], w_r[l, cbk], [], [kwg], kwg)
                b.dma('pool', wg[:, 1, :], w_i[l, cbk], [], [kwg], kwg)
                if conv_in is None:
                    if first:
                        b.op('dve', lambda: nc.vector.memset(XC[:, 0:3], 0.0), [], ['XC'])
                    else:
                        b.op('dve', lambda cbk=cbk: nc.vector.tensor_copy(XC[:, 0:3], CS[:, sq_, l, cbk, :]), ['CS'], ['XC'])
                else:
                    for j3 in range(3):
                        b.dma('sp', XC[:, j3:j3 + 1], conv_in[l, j3, cbk * 128:(cbk + 1) * 128].rearrange("(p o) -> p o", o=1), [], ['XC'], 'XC')
                b.op('act', lambda px=px: nc.scalar.copy(out=XC[:, 3:3 + T], in_=PS[px][:, 0:T]), [psk(px)], ['XC'])
                b.op('dve', lambda cbk=cbk: nc.vector.tensor_copy(CS[:, sq_, l, cbk, :], XC[:, T:T + 3]), ['XC'], ['CS'])
                if last:
                    for j3 in range(3):
                        b.dma('sp', conv_out[l, j3, cbk * 128:(cbk + 1) * 128].rearrange("(p o) -> p o", o=1), XC[:, T + j3:T + j3 + 1], ['XC'], [], 'XCst')
                b.op('dve', lambda cbk=cbk: nc.vector.tensor_scalar(XCV[:, 0:T], XC[:, 0:T], CWT[:, l, 0, cbk:cbk + 1], CBT[:, l, cbk:cbk + 1],
                                                                   ALU.mult, ALU.add), ['XC', 'CWT', 'CBT'], ['XCV'])
                for j in range(1, 4):
                    b.op('dve', lambda cbk=cbk, j=j: nc.vector.scalar_tensor_tensor(XCV[:, 0:T], XC[:, j:j + T], CWT[:, l, j, cbk:cbk + 1], XCV[:, 0:T],
                                                                                   ALU.mult, ALU.add), ['XC', 'CWT', 'XCV'], ['XCV'])
                if cbk + 1 < 8:
                    lnx = lstage(cbk + 1)
                b.op('act', lambda: nc.scalar.copy(out=XCV16[:, 0:T], in_=XCV[:, 0:T]), ['XCV'], ['XCV16'])
                pr = bank('sm')
                b.pe_group([lambda pr=pr, wg=wg: nc.tensor.matmul(PS[pr][:, 0:T], wg[:, 0, :], XCV16[:, 0:T], start=True, stop=True)],
                           [kwg, 'XCV16'], [psk(pr)])
                pg = bank('sm')
                b.pe_group([lambda pg=pg, wg=wg: nc.tensor.matmul(PS[pg][:, 0:T], wg[:, 1, :], XCV16[:, 0:T], start=True, stop=True)],
                           [kwg, 'XCV16'], [psk(pg)])
                A_, Bt, Ig = EB[0], EB[1], LNB
                b.op('act', lambda pr=pr, cbk=cbk: nc.scalar.activation(out=A_[:, 0:T], in_=PS[pr][:, 0:T], func=AF.Sigmoid, bias=BRT[:, l, cbk:cbk + 1]),
                     [psk(pr), 'BRT'], [('EB', 0)])
                b.op('act', lambda pg=pg, cbk=cbk: nc.scalar.activation(out=Ig[:, 0:T], in_=PS[pg][:, 0:T], func=AF.Sigmoid, bias=BIT[:, l, cbk:cbk + 1]),
                     [psk(pg), 'BIT'], ['LNB'])
                b.op('act', lambda cbk=cbk: nc.scalar.activation(out=A_[:, 0:T], in_=A_[:, 0:T], func=AF.Exp, scale=LCT[:, l, cbk:cbk + 1]),
                     [('EB', 0), 'LCT'], [('EB', 0)])
                b.op('dve', lambda: nc.vector.tensor_tensor(Bt[:, 0:T], A_[:, 0:T], A_[:, 0:T], ALU.mult), [('EB', 0)], [('EB', 1)])
                b.op('dve', lambda: nc.vector.tensor_scalar(Bt[:, 0:T], Bt[:, 0:T], -1.0, 1.0, ALU.mult, ALU.add), [('EB', 1)], [('EB', 1)])
                b.op('act', lambda: nc.scalar.activation(out=Bt[:, 0:T], in_=Bt[:, 0:T], func=AF.Sqrt), [('EB', 1)], [('EB', 1)])
                b.op('dve', lambda: nc.vector.tensor_tensor(Ig[:, 0:T], Ig[:, 0:T], XCV[:, 0:T], ALU.mult), ['LNB', 'XCV'], ['LNB'])
                b.op('dve', lambda: nc.vector.tensor_tensor(Bt[:, 0:T], Bt[:, 0:T], Ig[:, 0:T], ALU.mult), [('EB', 1), 'LNB'], [('EB', 1)])
                if lru_in is not None:
                    b.dma('sp', HS[:, sq_, l, cbk:cbk + 1], lru_in[l, cbk * 128:(cbk + 1) * 128].rearrange("(p o) -> p o", o=1), [], ['HS'], 'HS')
                b.op('dve', lambda cbk=cbk: nc.vector.tensor_tensor_scan(BL[:, 0:T], A_[:, 0:T], Bt[:, 0:T], HS[:, sq_, l, cbk:cbk + 1], ALU.mult, ALU.add),
                     [('EB', 0), ('EB', 1), 'HS'], ['BL'])
                b.op('dve', lambda cbk=cbk: nc.vector.tensor_copy(HS[:, sq_, l, cbk:cbk + 1], BL[:, T - 1:T]), ['BL'], ['HS'])
                if last:
                    b.dma('sp', lru_out[l, cbk * 128:(cbk + 1) * 128].rearrange("(p o) -> p o", o=1), HS[:, sq_, l, cbk:cbk + 1], ['HS'], [], 'HSst')
                b.op('act', lambda pc=pc: nc.scalar.activation(out=T1[1][:, 0:T], in_=PS[pc][:, 0:T], func=AF.Gelu), [psk(pc)], [('T1', 1)])
                b.op('dve', lambda cbk=cbk: nc.vector.tensor_tensor(O[:, 24 + cbk, 0:T], BL[:, 0:T], T1[1][:, 0:T], ALU.mult),
                     ['BL', ('T1', 1)], [('O', 24 + cbk)])
            okeys = [('O', c) for c in range(NCH)]
            for oc2 in range(16):
                s = load_w([(w_out[l][:, oc2 * 256:(oc2 + 1) * 256], 0)])
                for cc in range(2):
                    pi = proj_fm(s, cc * 128, 128, T, O, okeys)
                    c = oc2 * 2 + cc
                    b.op('dve', lambda pi=pi, c=c: nc.vector.tensor_tensor(X[:, c, 0:T], X[:, c, 0:T], PS[pi][:, 0:T], ALU.add),
                         [psk(pi), ('X', c)], [('X', c)])
            rmsnorm(LN2T[:, l, :], 'LN2T', T)
            for blk in range(8):
                hb = O[:, (blk % 2) * 16:(blk % 2) * 16 + 16, :]
                hkeys = [('O', (blk % 2) * 16 + j) for j in range(16)]
                for j in range(8):
                    s = load_w([(w_up[l][:, blk * 2048 + j * 256:blk * 2048 + (j + 1) * 256], 0)])
                    for cc in range(2):
                        pi = proj_fm(s, cc * 128, 128, T, XN, ['XN'])
                        t1 = T1[cc]
                        b.op('act', lambda pi=pi, t1=t1: nc.scalar.activation(out=t1[:, 0:T], in_=PS[pi][:, 0:T], func=AF.Relu), [psk(pi)], [('T1', cc)])
                        b.op('dve', lambda t1=t1, j=j, cc=cc, hb=hb: nc.vector.tensor_tensor(hb[:, j * 2 + cc, 0:T], t1[:, 0:T], t1[:, 0:T], ALU.mult),
                             [('T1', cc)], [hkeys[j * 2 + cc]])
                for oc4 in range(8):
                    s = load_w([(w_down[l][blk * 2048:(blk + 1) * 2048, oc4 * 512:(oc4 + 1) * 512], 0)], nk=16, wide=True)
                    for cc in range(4):
                        pi = proj_fm(s, cc * 128, 128, T, hb, hkeys, nk=16, wide=True)
                        c = oc4 * 4 + cc
                        b.op('dve', lambda pi=pi, c=c: nc.vector.tensor_tensor(X[:, c, 0:T], X[:, c, 0:T], PS[pi][:, 0:T], ALU.add),
                             [psk(pi), ('X', c)], [('X', c)])

        for p in range(npass):
            wstate['cached'] = p > 0
            load_x(xp[p * TP:(p + 1) * TP, :], TP)
            for l in range(2):
                layer(l, TP, 0, p == 0, p == npass - 1, None if p == 0 else glap, glap, None, convp, None, lrup, None)
            store_y(yp[p * TP:(p + 1) * TP, :], TP)
        for sp_ in range(nsp):
            wstate['cached'] = npass > 0 or sp_ > 0
            load_x(xs[sp_], 32)
            for l in range(2):
                layer(l, 32, 1, True, True, sgla[sp_], glas[sp_], sconv[sp_], convs[sp_], slru[sp_], lrus[sp_], vs[sp_])
            store_y(ys[sp_], 32)
        b.finish()
    return nc


def _consts():
    t = np.arange(512)
    cmask = np.broadcast_to((t % 128 != 0).astype(np.float32)[None, :], (128, 512)).copy()
    j = np.arange(128)
    mu = (j[:, None] <= j[None, :]).astype(np.float32)
    masks = ((j[:, None] // 64) == (np.arange(256)[None, :] // 128)).astype(np.float32)
    rowmask = ((j[:, None] // 64) == np.arange(2)[None, :]).astype(np.float32)
    ident = np.eye(128, dtype=np.float32)
    return dict(cmask=cmask, mu=mu, masks=masks, rowmask=rowmask, ident=ident)


_PROG = {}

CORES = 8
PROMPT_AT = {0: 0, 4: 1}
NSP = 1


def run(inputs, npass, cores=None, prompt_at=None, nsp=None, trace=False):
    cores = CORES if cores is None else cores
    prompt_at = PROMPT_AT if prompt_at is None else prompt_at
    nsp = NSP if nsp is None else nsp
    key = (npass, nsp)
    if key not in _PROG:
        _PROG[key] = build_program(npass, nsp)
    nc = _PROG[key]
    f = lambda a: np.ascontiguousarray(np.asarray(a, dtype=np.float32))
    shared = {k: f(inputs[k]) for k in ('ln1', 'w_in', 'w_alpha2', 'b_alpha', 'g_onorm', 'w_spatial', 'b_spatial', 'conv_w', 'conv_b',
                                        'w_rgate', 'b_rgate', 'w_igate', 'b_igate', 'lru_lambda', 'w_out', 'ln2', 'w_up', 'w_down',
                                        'ln_final')}
    shared.update(_consts())
    xpf = f(inputs['x_prompt'])
    xsf = f(inputs['x_sample'])
    sg, scv, slr = f(inputs['state_gla']), f(inputs['state_conv']), f(inputs['state_lru'])
    zeros_p = np.zeros((npass * TP, D), np.float32)
    nsamp = min(xsf.shape[0], cores * nsp)
    in_maps = []
    for c in range(cores):
        m = dict(shared)
        m['xp'] = np.ascontiguousarray(xpf[prompt_at[c], :npass * TP]) if c in prompt_at else zeros_p
        ids = [min(c * nsp + j, xsf.shape[0] - 1) for j in range(nsp)]
        m['xs'] = np.ascontiguousarray(xsf[ids])
        m['sgla'] = np.ascontiguousarray(np.stack([sg[:, i] for i in ids]))
        m['sconv'] = np.ascontiguousarray(np.stack([scv[:, i] for i in ids]))
        m['slru'] = np.ascontiguousarray(np.stack([slr[:, i] for i in ids]))
        in_maps.append(m)
    res = run_bass_kernel_spmd(nc, in_maps, core_ids=list(range(cores)), **({'trace': True} if trace else {}))
    r = res.results
    pcs = sorted(prompt_at, key=lambda c: prompt_at[c])
    y_prompt = np.stack([r[c]['yp'] for c in pcs])
    gla_p = np.stack([r[c]['glap'] for c in pcs], axis=1)
    conv_p = np.stack([r[c]['convp'] for c in pcs], axis=1)
    lru_p = np.stack([r[c]['lrup'] for c in pcs], axis=1)
    samp = [(i // nsp, i % nsp) for i in range(nsamp)]
    y_sample = np.stack([r[c]['ys'][j] for c, j in samp])
    gla_s = np.stack([r[c]['glas'][j] for c, j in samp], axis=1)
    conv_s = np.stack([r[c]['convs'][j] for c, j in samp], axis=1)
    lru_s = np.stack([r[c]['lrus'][j] for c, j in samp], axis=1)
    v_s = np.stack([r[c]['vs'][j] for c, j in samp], axis=1)
    outs = (y_prompt, y_sample, gla_p, conv_p, lru_p, gla_s, conv_s, lru_s, v_s)
    return tuple(np.ascontiguousarray(o.astype(np.float32)) for o in outs), res


def kernel(**inputs):
    outs, _ = run(inputs, 4096 // TP)
    return outs
```

```python
import contextlib
import numpy as np
import concourse.bass as bass
import concourse.mybir as mybir
from concourse.bass_utils import run_bass_kernel_spmd

F32 = mybir.dt.float32
BF16 = mybir.dt.bfloat16
AF = mybir.ActivationFunctionType
ALU = mybir.AluOpType

D = 4096
NCH = 32
PW = 10256
DFF = 16384
Q0, K0, V0, G0, A0, U0, VB0, XC0, GC0 = 0, 1024, 2048, 4096, 6144, 6160, 7184, 8208, 9232
EPS = 1e-6
TP = 512
PROMPT_CORES = (0, 4)
N_CORES = 8


class B:
    def __init__(self, nc, es):
        self.nc = nc
        self.es = es
        self.E = {'pe': nc.tensor, 'act': nc.scalar, 'dve': nc.vector, 'pool': nc.gpsimd, 'sp': nc.sync}
        self.sem = {}
        self.cnt = {}
        for e in ('pe', 'act', 'dve'):
            self.sem[e] = es.enter_context(nc.semaphore('s_' + e))
            self.cnt[e] = 0
        self.known = {e: {} for e in self.E}
        self.res = {}
        self.dsem = {}
        self.dcnt = {}
        self.alias = {}
        self.expand = {}

    def _canon(self, keys):
        out = []
        for k in keys:
            k = self.alias.get(k, k)
            if k in self.expand:
                out.extend(self.expand[k])
            else:
                out.append(k)
        return out

    def _deps(self, reads, writes):
        reads, writes = self._canon(reads), self._canon(writes)
        deps = {}

        def add(k, v):
            if deps.get(k, 0) < v:
                deps[k] = v
        for r in reads:
            st = self.res.get(r)
            if st and st[0]:
                add(*st[0])
        for w in writes:
            st = self.res.get(w)
            if st:
                if st[0]:
                    add(*st[0])
                for k, v in st[1].items():
                    add(k, v)
        return deps

    def _semobj(self, k):
        return self.sem[k] if k in self.sem else self.dsem[k]

    def _wait(self, eng, deps):
        kn = self.known[eng]
        for k, v in deps.items():
            if eng == 'pe' and k == 'pe':
                continue
            if kn.get(k, 0) >= v:
                continue
            self.E[eng].wait_ge(self._semobj(k), v)
            kn[k] = v

    def _mark(self, tok, reads, writes):
        reads, writes = self._canon(reads), self._canon(writes)
        k, v = tok
        for r in reads:
            st = self.res.setdefault(r, [None, {}])
            if st[1].get(k, 0) < v:
                st[1][k] = v
        for w in writes:
            self.res[w] = [tok, {}]

    def op(self, eng, fn, reads=(), writes=()):
        self._wait(eng, self._deps(reads, writes))
        inst = fn()
        self.cnt[eng] += 1
        inst.then_inc(self.sem[eng], 1)
        self._mark((eng, self.cnt[eng]), reads, writes)

    def pe_group(self, fns, reads=(), writes=(), per_fn_reads=None):
        self._wait('pe', self._deps(reads, writes))
        inst = None
        for i, fn in enumerate(fns):
            if per_fn_reads is not None:
                self._wait('pe', self._deps(per_fn_reads[i], []))
            inst = fn()
        if per_fn_reads is not None:
            reads = list(reads) + [k for ks in per_fn_reads for k in ks]
        self.cnt['pe'] += 1
        inst.then_inc(self.sem['pe'], 1)
        self._mark(('pe', self.cnt['pe']), reads, writes)

    def dma(self, q, out, in_, reads, writes, stream, slow=False):
        stream = self.alias.get(stream, stream)
        self._wait(q, self._deps(reads, writes))
        if stream not in self.dsem:
            self.dsem[stream] = self.es.enter_context(self.nc.semaphore('d_%d' % len(self.dsem)))
            self.dcnt[stream] = 0
        inst = self.E[q].dma_start(out=out, in_=in_, allow_slow_non_contiguous=True) if slow else self.E[q].dma_start(out=out, in_=in_)
        self.dcnt[stream] += 16
        inst.then_inc(self.dsem[stream], 16)
        self._mark((stream, self.dcnt[stream]), reads, writes)

    def finish(self):
        for k, v in self.dcnt.items():
            if v and self.known['sp'].get(k, 0) < v:
                self.E['sp'].wait_ge(self.dsem[k], v)
                self.known['sp'][k] = v


def build_program(npass, nsp=1):
    nc = bass.Bass("TRN2", target_bir_lowering=False)

    def din(name, shape):
        return nc.dram_tensor(name, shape, F32, kind="ExternalInput").ap()

    def dout(name, shape):
        return nc.dram_tensor(name, shape, F32, kind="ExternalOutput").ap()

    xp = din("xp", [npass * TP, D])
    xs = din("xs", [nsp, 32, D])
    sgla = din("sgla", [nsp, 2, 16, 64, 128])
    sconv = din("sconv", [nsp, 2, 3, 1024])
    slru = din("slru", [nsp, 2, 1024])
    ln1 = din("ln1", [2, D])
    w_in = din("w_in", [2, D, PW])
    w_a2 = din("w_alpha2", [2, 16, 1024])
    b_al = din("b_alpha", [2, 1024])
    g_on = din("g_onorm", [2, 128])
    w_sp = din("w_spatial", [2, 8, 128, 128])
    b_sp = din("b_spatial", [2, 8, 128])
    cw = din("conv_w", [2, 4, 1024])
    cb = din("conv_b", [2, 1024])
    w_r = din("w_rgate", [2, 8, 128, 128])
    b_r = din("b_rgate", [2, 1024])
    w_i = din("w_igate", [2, 8, 128, 128])
    b_i = din("b_igate", [2, 1024])
    lam = din("lru_lambda", [2, 1024])
    w_out = din("w_out", [2, D, D])
    ln2 = din("ln2", [2, D])
    w_up = din("w_up", [2, D, DFF])
    w_down = din("w_down", [2, DFF, D])
    lnf = din("ln_final", [D])
    cmask_d = din("cmask", [128, 512])
    mu_d = din("mu", [128, 128])
    masks_d = din("masks", [128, 256])
    rm_d = din("rowmask", [128, 2])
    ident_d = din("ident", [128, 128])

    yp = dout("yp", [npass * TP, D])
    ys = dout("ys", [nsp, 32, D])
    glap = dout("glap", [2, 16, 64, 128])
    convp = dout("convp", [2, 3, 1024])
    lrup = dout("lrup", [2, 1024])
    glas = dout("glas", [nsp, 2, 16, 64, 128])
    convs = dout("convs", [nsp, 2, 3, 1024])
    lrus = dout("lrus", [nsp, 2, 1024])
    vs = dout("vs", [nsp, 2, 32, 1024])
    NBLK = 249
    WCT = [nc.dram_tensor("wcache%d" % i, [100, 128, NCH * 256], BF16, kind="Internal").ap() for i in range(5)]

    class _WC:
        def __getitem__(self, key):
            l, idx = key[0], key[1]
            g = l * NBLK + idx
            return WCT[g // 100][(g % 100,) + tuple(key[2:])]
    WC = _WC()

    es = contextlib.ExitStack()
    with es:
        b = B(nc, es)

        def sb(name, shape, dt):
            return es.enter_context(nc.sbuf_tensor(name, shape, dt))

        X = sb("X", [128, NCH, TP], F32)
        XN = sb("XN", [128, NCH, TP], BF16)
        O = sb("O", [128, NCH, TP], BF16)
        STG = XN.bitcast(F32)
        WB = [sb("WB%d" % i, [128, NCH, 256], BF16) for i in range(2)]
        LN1T = sb("LN1T", [128, 2, NCH], F32)
        LN2T = sb("LN2T", [128, 2, NCH], F32)
        LNFT = sb("LNFT", [128, NCH], F32)
        NBAL = sb("NBAL", [128, 2, 8], F32)
        GON = sb("GON", [128, 2], F32)
        CWT = sb("CWT", [128, 2, 4, 8], F32)
        CBT = sb("CBT", [128, 2, 8], F32)
        BRT = sb("BRT", [128, 2, 8], F32)
        BIT = sb("BIT", [128, 2, 8], F32)
        LCT = sb("LCT", [128, 2, 8], F32)
        ONE = sb("ONE", [128, 1], F32)
        CM = sb("CM", [128, 512], BF16)
        MU = sb("MU", [128, 128], F32)
        MASKS = sb("MASKS", [128, 256], F32)
        RM = sb("RM", [128, 2], F32)
        IDENT = sb("IDENT", [128, 128], F32)
        ONES16 = sb("ONES16", [128, 128], BF16)
        WST = sb("WST", [128, 2, 8, 128], BF16)
        BSP2 = sb("BSP2", [33, 2, 8, 128], BF16)
        WA2 = sb("WA2", [16, 1024], BF16)
        WG = [sb("WG%d" % i, [128, 2, 128], BF16) for i in range(2)]
        HS = sb("HS", [128, 2, 2, 8], F32)
        CS = sb("CS", [128, 2, 2, 8, 3], F32)
        ALR = sb("ALR", [16, TP], BF16)
        LNB = sb("LNB", [128, TP], F32)
        BL = sb("BL", [128, TP], F32)
        EBa = sb("EB0", [128, TP], F32)
        E1 = sb("E1", [128, TP], F32)
        QT = sb("QT", [128, TP], BF16)
        KZ = sb("KZ", [128, 2, TP], BF16)
        KE = sb("KE", [128, TP], F32)
        VBF = KE[:, :].rearrange("p (a d) -> p a d", a=4)
        KET = sb("KET", [128, 4, 128], BF16)
        VT = sb("VT", [128, 4, 256], BF16)
        SG = sb("SG", [128, 2, TP], BF16)
        S32 = sb("S32", [128, 256], F32)
        S16 = sb("S16", [128, 256], BF16)
        TMPS = sb("TMPS", [128, 256], F32)
        ATT = [sb("ATT%d" % i, [128, 128], BF16) for i in range(2)]
        SQ = [sb("SQ%d" % i, [128, TP], BF16) for i in range(2)]
        RS = sb("RS", [128, TP], F32)
        T1a = sb("T1a", [128, TP + 8], F32)

        EB = [EBa, KE]
        T1 = [T1a, RS]
        XC, XCV, XCV16 = T1a, E1, QT
        b.alias = {('EB', 1): 'KE', ('T1', 1): 'RS', 'XC': ('T1', 0), 'XCV': 'E1', 'XCV16': 'QT', 'VBF': 'KE'}
        b.expand = {'XN': [('XN', c) for c in range(NCH)]}
        PS = [es.enter_context(nc.psum_tensor("PS%d" % i, [128, 512], F32)) for i in range(8)]
        rot = {'mm': [0, [0, 1, 2, 3]], 'o': [0, [4, 5]], 'sm': [0, [6, 7]]}

        def bank(kind):
            r = rot[kind]
            i = r[1][r[0] % len(r[1])]
            r[0] += 1
            return i

        def psk(i):
            return ('ps', i)

        cres = []

        def cload(dst_ap, src_ap, key):
            b.dma('sp', dst_ap, src_ap, [], [key], 'const', slow=True)
            cres.append(key)

        for l in range(2):
            cload(LN1T[:, l, :], ln1[l].rearrange("(c p) -> p c", p=128), 'LN1T')
            cload(LN2T[:, l, :], ln2[l].rearrange("(c p) -> p c", p=128), 'LN2T')
            cload(NBAL[:, l, :], b_al[l].rearrange("(c p) -> p c", p=128), 'NBAL')
            cload(GON[:, l:l + 1], g_on[l].rearrange("(p o) -> p o", o=1), 'GON')
            for j in range(4):
                cload(CWT[:, l, j, :], cw[l, j].rearrange("(c p) -> p c", p=128), 'CWT')
            cload(CBT[:, l, :], cb[l].rearrange("(c p) -> p c", p=128), 'CBT')
            cload(BRT[:, l, :], b_r[l].rearrange("(c p) -> p c", p=128), 'BRT')
            cload(BIT[:, l, :], b_i[l].rearrange("(c p) -> p c", p=128), 'BIT')
            cload(LCT[:, l, :], lam[l].rearrange("(c p) -> p c", p=128), 'LCT')
        cload(LNFT[:, :], lnf.rearrange("(c p) -> p c", p=128), 'LNFT')
        b.dma('pool', CM[:, :], cmask_d[:, :], [], ['CM'], 'CMld')
        cload(MU[:, :], mu_d[:, :], 'MU')
        cload(MASKS[:, :], masks_d[:, :], 'MASKS')
        cload(RM[:, :], rm_d[:, :], 'RM')
        cload(IDENT[:, :], ident_d[:, :], 'IDENT')
        for key in set(cres):
            b.res[key] = [('const', b.dcnt['const']), {}]

        b.op('dve', lambda: nc.vector.memset(ONE[:], 1.0), [], ['ONE'])
        b.op('dve', lambda: nc.vector.memset(ONES16[:], 1.0), [], ['ONES16'])
        b.op('dve', lambda: nc.vector.memset(BSP2[:].rearrange("p a h i -> p (a h i)"), 0.0), [], ['BSP2'])
        b.op('dve', lambda: nc.vector.memset(HS[:].rearrange("p a l c -> p (a l c)"), 0.0), [], ['HS'])
        b.op('dve', lambda: nc.vector.memset(CS[:].rearrange("p a l c j -> p (a l c j)"), 0.0), [], ['CS'])
        b.op('dve', lambda: nc.vector.tensor_scalar(NBAL[:].rearrange("p l c -> p (l c)"), NBAL[:].rearrange("p l c -> p (l c)"),
                                                    -1.0, None, ALU.mult), ['NBAL'], ['NBAL'])
        lct2 = LCT[:].rearrange("p l c -> p (l c)")
        b.op('act', lambda: nc.scalar.activation(out=lct2, in_=lct2, func=AF.Exp, scale=-1.0), ['LCT'], ['LCT'])
        b.op('act', lambda: nc.scalar.activation(out=lct2, in_=lct2, func=AF.Ln, bias=ONE[:, 0:1]), ['LCT', 'ONE'], ['LCT'])
        b.op('dve', lambda: nc.vector.tensor_scalar(lct2, lct2, -8.0, None, ALU.mult), ['LCT'], ['LCT'])
        bsp2f = BSP2[:].rearrange("p a h i -> p (a h i)")
        bspflat = b_sp.rearrange("l h i -> (l h i)").rearrange("(o n) -> o n", o=1)
        for q in range(4):
            qs = slice(q * 512, (q + 1) * 512)
            b.dma('sp', EB[0][0:1, :], bspflat[:, qs], [], [('EB', 0)], ('EB', 0))
            b.dma('sp', EB[0][32:33, :], bspflat[:, qs], [], [('EB', 0)], ('EB', 0))
            b.op('dve', lambda qs=qs: nc.vector.tensor_copy(bsp2f[0:1, qs], EB[0][0:1, :]), [('EB', 0)], ['BSP2'])
            b.op('dve', lambda qs=qs: nc.vector.tensor_copy(bsp2f[32:33, qs], EB[0][32:33, :]), [('EB', 0)], ['BSP2'])
            b.op('dve', lambda qs=qs: nc.vector.tensor_copy(EB[1][32:33, :], bsp2f[32:33, qs]), ['BSP2'], [('EB', 1)])
            b.op('dve', lambda: nc.vector.tensor_tensor(EB[1][32:33, :], EB[0][32:33, :], EB[1][32:33, :], ALU.subtract),
                 [('EB', 0), ('EB', 1)], [('EB', 1)])
            b.op('dve', lambda qs=qs: nc.vector.tensor_copy(bsp2f[32:33, qs], EB[1][32:33, :]), [('EB', 1)], ['BSP2'])
        for l in range(2):
            for h in range(8):
                stg = EB[(l * 8 + h) % 2]
                k = ('EB', (l * 8 + h) % 2)
                b.dma('sp', stg[:, 0:128], w_sp[l, h], [], [k], k)
                pi = bank('sm')
                b.pe_group([lambda pi=pi, stg=stg: nc.tensor.transpose(out=PS[pi][:, 0:128], in_=stg[:, 0:128], identity=IDENT[:, :])],
                           [k, 'IDENT'], [psk(pi)])
                b.op('dve', lambda pi=pi, l=l, h=h: nc.vector.tensor_tensor(WST[:, l, h, :], PS[pi][:, 0:128], MU[:, :], ALU.mult),
                     [psk(pi), 'MU'], ['WST'])

        wrot = [0]
        wstate = {'layer': 0, 'idx': 0, 'cached': False}

        WBW = [w[:].rearrange("p c n -> p (c n)").rearrange("p (k n) -> p k n", k=16, n=512) for w in WB]

        def load_w(pieces, nk=NCH, wide=False):
            s = wrot[0] % 2
            wrot[0] += 1
            l, idx = wstate['layer'], wstate['idx']
            wstate['idx'] += 1
            assert idx < NBLK
            view = WBW[s] if wide else WB[s]
            nimg = 16 * 512 if wide else nk * 256
            img = WB[s][:].rearrange("p c n -> p (c n)")[:, 0:nimg]
            if wstate['cached']:
                b.dma('pool', img, WC[l, idx, :, 0:nimg], [('wc', l, idx)], [('wb', s)], ('wb', s))
            else:
                for src, off in pieces:
                    n = src.shape[1]
                    b.dma('pool', view[:, 0:nk, off:off + n], src.rearrange("(c p) n -> p c n", p=128),
                          [], [('wb', s)], ('wb', s))
                b.dma('sp', WC[l, idx, :, 0:nimg], img, [('wb', s)], [('wc', l, idx)], ('wbst', s))
            return s

        def proj_fm(s, off, ncols, T, rhs_buf, rhs_keys, nk=NCH, wide=False):
            pi = bank('mm')
            wv = WBW[s] if wide else WB[s]
            fns = [(lambda kc=kc: nc.tensor.matmul(PS[pi][0:ncols, 0:T], wv[:, kc, off:off + ncols], rhs_buf[:, kc, 0:T],
                                                   start=(kc == 0), stop=(kc == nk - 1))) for kc in range(nk)]
            if list(rhs_keys) == ['XN']:
                b.pe_group(fns, [('wb', s)], [psk(pi)], per_fn_reads=[[('XN', kc)] for kc in range(nk)])
            else:
                b.pe_group(fns, [('wb', s)] + list(rhs_keys), [psk(pi)])
            return pi

        def proj_tm(s, off, ncols, tt, tp, pi, po):
            fns = [(lambda kc=kc: nc.tensor.matmul(PS[pi][0:tp, po:po + ncols], XN[:, kc, tt * 128:tt * 128 + tp],
                                                   WB[s][:, kc, off:off + ncols], start=(kc == 0), stop=(kc == NCH - 1)))
                   for kc in range(NCH)]
            b.pe_group(fns, [('wb', s), 'XN'], [psk(pi)])

        def rmsnorm(lnT, lnkey, T):
            pi = bank('sm')
            for c in range(NCH):
                sq = SQ[c % 2]
                b.op('act', lambda c=c, sq=sq: nc.scalar.activation(out=sq[:, 0:T], in_=X[:, c, 0:T], func=AF.Square),
                     [('X', c)], [('SQ', c % 2)])
                b.op('pe', lambda c=c, sq=sq: nc.tensor.matmul(PS[pi][:, 0:T], ONES16[:, :], sq[:, 0:T], start=(c == 0), stop=(c == NCH - 1)),
                     [('SQ', c % 2), 'ONES16'], [psk(pi)])
            b.op('dve', lambda: nc.vector.tensor_scalar(RS[:, 0:T], PS[pi][:, 0:T], 1.0 / D, EPS, ALU.mult, ALU.add), [psk(pi)], ['RS'])
            b.op('act', lambda: nc.scalar.activation(out=RS[:, 0:T], in_=RS[:, 0:T], func=AF.Sqrt), ['RS'], ['RS'])
            b.op('dve', lambda: nc.vector.reciprocal(RS[:, 0:T], RS[:, 0:T]), ['RS'], ['RS'])
            for c in range(NCH):
                b.op('dve', lambda c=c: nc.vector.scalar_tensor_tensor(XN[:, c, 0:T], X[:, c, 0:T], lnT[:, c:c + 1], RS[:, 0:T],
                                                                       ALU.mult, ALU.mult),
                     [('X', c), 'RS', lnkey], [('XN', c)])

        def load_x(src, T):
            tp = min(T, 128)
            ntt = (T + 127) // 128
            for tt in range(ntt):
                stg = STG[:, (tt % 2) * 16:(tt % 2) * 16 + 16, :].rearrange("p a b -> p (a b)")
                k = ('stg', tt % 2)
                b.dma('sp', stg[0:tp, :], src[tt * 128:tt * 128 + tp, :], [], [k, 'XN'], k)
                for c0 in range(0, NCH, 4):
                    pi = bank('sm')
                    fns = [(lambda j=j: nc.tensor.transpose(out=PS[pi][:, j * 128:j * 128 + tp], in_=stg[0:tp, (c0 + j) * 128:(c0 + j + 1) * 128],
                                                            identity=IDENT[0:tp, 0:tp])) for j in range(4)]
                    b.pe_group(fns, [k, 'IDENT'], [psk(pi)])
                    src_v = PS[pi][:, :].rearrange("p (a t) -> p a t", a=4)[:, :, 0:tp]
                    eng = 'act' if (c0 // 4) % 2 else 'dve'
                    if eng == 'act':
                        b.op('act', lambda c0=c0, src_v=src_v: nc.scalar.copy(out=X[:, c0:c0 + 4, tt * 128:tt * 128 + tp], in_=src_v),
                             [psk(pi)], [('X', c0 + j) for j in range(4)])
                    else:
                        b.op('dve', lambda c0=c0, src_v=src_v: nc.vector.tensor_copy(X[:, c0:c0 + 4, tt * 128:tt * 128 + tp], src_v),
                             [psk(pi)], [('X', c0 + j) for j in range(4)])

        def store_y(dst, T):
            tp = min(T, 128)
            ntt = (T + 127) // 128
            pi = bank('sm')
            for c in range(NCH):
                sq = SQ[c % 2]
                b.op('act', lambda c=c, sq=sq: nc.scalar.activation(out=sq[:, 0:T], in_=X[:, c, 0:T], func=AF.Square),
                     [('X', c)], [('SQ', c % 2)])
                b.op('pe', lambda c=c, sq=sq: nc.tensor.matmul(PS[pi][:, 0:T], ONES16[:, :], sq[:, 0:T], start=(c == 0), stop=(c == NCH - 1)),
                     [('SQ', c % 2), 'ONES16'], [psk(pi)])
            b.op('dve', lambda: nc.vector.tensor_scalar(RS[:, 0:T], PS[pi][:, 0:T], 1.0 / D, EPS, ALU.mult, ALU.add), [psk(pi)], ['RS'])
            b.op('act', lambda: nc.scalar.activation(out=RS[:, 0:T], in_=RS[:, 0:T], func=AF.Sqrt), ['RS'], ['RS'])
            b.op('dve', lambda: nc.vector.reciprocal(RS[:, 0:T], RS[:, 0:T]), ['RS'], ['RS'])
            for c in range(NCH):
                b.op('dve', lambda c=c: nc.vector.scalar_tensor_tensor(X[:, c, 0:T], X[:, c, 0:T], LNFT[:, c:c + 1], RS[:, 0:T],
                                                                       ALU.mult, ALU.mult),
                     [('X', c), 'RS', 'LNFT'], [('X', c)])
            for tt in range(ntt):
                stg = STG[:, (tt % 2) * 16:(tt % 2) * 16 + 16, :].rearrange("p a b -> p (a b)")
                k = ('stg', tt % 2)
                for c0 in range(0, NCH, 4):
                    pi = bank('sm')
                    fns = [(lambda j=j: nc.tensor.transpose(out=PS[pi][0:tp, j * 128:(j + 1) * 128], in_=X[:, c0 + j, tt * 128:tt * 128 + tp],
                                                            identity=IDENT[:, :])) for j in range(4)]
                    b.pe_group(fns, [('X', c0 + j) for j in range(4)] + ['IDENT'], [psk(pi)])
                    if (c0 // 4) % 2:
                        b.op('act', lambda c0=c0, pi=pi: nc.scalar.copy(out=stg[0:tp, c0 * 128:(c0 + 4) * 128], in_=PS[pi][0:tp, :]),
                             [psk(pi)], [k, 'XN'])
                    else:
                        b.op('dve', lambda c0=c0, pi=pi: nc.vector.tensor_copy(stg[0:tp, c0 * 128:(c0 + 4) * 128], PS[pi][0:tp, :]),
                             [psk(pi)], [k, 'XN'])
                b.dma('sp', dst[tt * 128:tt * 128 + tp, :], stg[0:tp, :], [k], [], k)

        def layer(l, T, sq_, first, last, gla_in, gla_out, conv_in, conv_out, lru_in, lru_out, vs_out):
            tp = min(T, 128)
            ntt = (T + 127) // 128
            CH = tp
            nchk = T // CH
            wstate['layer'] = l
            wstate['idx'] = 0
            rmsnorm(LN1T[:, l, :], 'LN1T', T)
            s = load_w([(w_in[l][:, A0:A0 + 16], 0)])
            pi = proj_fm(s, 0, 16, T, XN, ['XN'])
            b.op('act', lambda pi=pi: nc.scalar.copy(out=ALR[0:16, 0:T], in_=PS[pi][0:16, 0:T]), [psk(pi)], ['ALR'])
            b.dma('pool', WA2[:, :], w_a2[l], [], ['WA2'], 'WA2')
            def stage1(m):
                sA = load_w([(w_in[l][:, Q0 + m * 128:Q0 + (m + 1) * 128], 0), (w_in[l][:, K0 + m * 128:K0 + (m + 1) * 128], 128)])
                return proj_fm(sA, 0, 128, T, XN, ['XN']), proj_fm(sA, 128, 128, T, XN, ['XN'])

            nxt = stage1(0)
            for m in range(8):
                pq, pk = nxt
                pa = bank('sm')
                b.pe_group([lambda pa=pa, m=m: nc.tensor.matmul(PS[pa][:, 0:T], WA2[0:16, m * 128:(m + 1) * 128], ALR[0:16, 0:T], start=True, stop=True)],
                           ['WA2', 'ALR'], [psk(pa)])
                b.op('act', lambda pa=pa, m=m: nc.scalar.activation(out=LNB[:, 0:T], in_=PS[pa][:, 0:T], func=AF.Exp, scale=-1.0, bias=NBAL[:, l, m:m + 1]),
                     [psk(pa), 'NBAL'], ['LNB'])
                b.op('act', lambda: nc.scalar.activation(out=LNB[:, 0:T], in_=LNB[:, 0:T], func=AF.Ln, bias=ONE[:, 0:1]), ['LNB', 'ONE'], ['LNB'])
                b.op('dve', lambda: nc.vector.tensor_tensor_scan(BL[:, 0:T], CM[:, 0:T], LNB[:, 0:T], 0.0, ALU.mult, ALU.add),
                     ['LNB', 'CM'], ['BL'])
                b.op('act', lambda: nc.scalar.activation(out=E1[:, 0:T], in_=BL[:, 0:T], func=AF.Exp, scale=-1.0 / 16), ['BL'], ['E1'])
                b.op('act', lambda: nc.scalar.activation(out=EB[0][:, 0:T], in_=BL[:, 0:T], func=AF.Exp, scale=1.0 / 16), ['BL'], [('EB', 0)])
                blend = BL[:, CH - 1:T:CH].rearrange("p (c o) -> p c o", o=1).broadcast_to([128, nchk, CH])
                b.op('dve', lambda blend=blend: nc.vector.tensor_tensor(EB[1][:, 0:T].rearrange("p (c t) -> p c t", t=CH), blend,
                                                                       BL[:, 0:T].rearrange("p (c t) -> p c t", t=CH), ALU.subtract),
                     ['BL'], [('EB', 1)])
                b.op('act', lambda: nc.scalar.activation(out=EB[1][:, 0:T], in_=EB[1][:, 0:T], func=AF.Exp, scale=-1.0 / 16), [('EB', 1)], [('EB', 1)])
                b.op('dve', lambda pq=pq: nc.vector.scalar_tensor_tensor(QT[:, 0:T], PS[pq][:, 0:T], 0.125, E1[:, 0:T], ALU.mult, ALU.mult),
                     [psk(pq), 'E1'], ['QT'])
                for hh in range(2):
                    b.op('dve', lambda pk=pk, hh=hh: nc.vector.scalar_tensor_tensor(KZ[:, hh, 0:T], PS[pk][:, 0:T], RM[:, hh:hh + 1], EB[0][:, 0:T],
                                                                                   ALU.mult, ALU.mult),
                         [psk(pk), ('EB', 0), 'RM'], [('KZ', hh)])
                b.op('dve', lambda pk=pk: nc.vector.tensor_tensor(KE[:, 0:T], PS[pk][:, 0:T], EB[1][:, 0:T], ALU.mult), [psk(pk), ('EB', 1)], ['KE'])
                pt = bank('sm')
                b.pe_group([(lambda tt=tt: nc.tensor.transpose(out=PS[pt][0:tp, tt * 128:(tt + 1) * 128], in_=KE[:, tt * 128:tt * 128 + tp],
                                                               identity=IDENT[:, :])) for tt in range(ntt)], ['KE', 'IDENT'], [psk(pt)])
                b.op('act', lambda pt=pt: nc.scalar.copy(out=KET[0:tp, 0:ntt, :].rearrange("p a d -> p (a d)"), in_=PS[pt][0:tp, 0:ntt * 128]),
                     [psk(pt)], ['KET'])
                sB = load_w([(w_in[l][:, V0 + m * 256:V0 + (m + 1) * 256], 0)])
                for t2 in range(0, ntt, 2):
                    pv = bank('mm')
                    n2 = min(2, ntt - t2)
                    for j in range(n2):
                        proj_tm(sB, 0, 256, t2 + j, tp, pv, j * 256)
                    b.op('act', lambda pv=pv, t2=t2, n2=n2: nc.scalar.copy(out=VT[0:tp, t2:t2 + n2, :].rearrange("p a d -> p (a d)"),
                                                                         in_=PS[pv][0:tp, 0:n2 * 256]), [psk(pv)], ['VT'])
                sC = load_w([(w_in[l][:, G0 + m * 256:G0 + (m + 1) * 256], 0)])
                for hh in range(2):
                    pg = proj_fm(sC, hh * 128, 128, T, XN, ['XN'])
                    b.op('act', lambda pg=pg, hh=hh: nc.scalar.activation(out=SG[:, hh, 0:T], in_=PS[pg][:, 0:T], func=AF.Silu),
                         [psk(pg)], [('SG', hh)])
                if gla_in is None:
                    b.op('dve', lambda: nc.vector.memset(S32[:, :], 0.0), [], ['S32'])
                else:
                    if first:
                        b.op('dve', lambda: nc.vector.memset(S32[:, :], 0.0), [], ['S32'])
                    for hh in range(2):
                        b.dma('sp', S32[hh * 64:(hh + 1) * 64, hh * 128:(hh + 1) * 128], gla_in[l, 2 * m + hh],
                              [('glad', id(gla_in), l, 2 * m + hh)], ['S32'], 'S32')
                b.op('act', lambda: nc.scalar.copy(out=S16[:, :], in_=S32[:, :]), ['S32'], ['S16'])
                po = [bank('o'), bank('o')]
                for c in range(nchk):
                    cs = slice(c * CH, (c + 1) * CH)
                    for hh in range(2):
                        pa = bank('sm')
                        b.pe_group([lambda pa=pa, hh=hh, cs=cs: nc.tensor.matmul(PS[pa][0:CH, 0:CH], KZ[:, hh, cs], QT[:, cs], start=True, stop=True)],
                                   [('KZ', hh), 'QT'], [psk(pa)])
                        b.op('dve', lambda pa=pa, hh=hh: nc.vector.tensor_tensor(ATT[hh][0:CH, 0:CH], PS[pa][0:CH, 0:CH], MU[0:CH, 0:CH], ALU.mult),
                             [psk(pa), 'MU'], [('ATT', hh)])
                        b.pe_group([lambda hh=hh, c=c, cs=cs: nc.tensor.matmul(PS[po[hh]][:, cs], VT[0:CH, c, hh * 128:(hh + 1) * 128], ATT[hh][0:CH, 0:CH],
                                                                               start=True, stop=False),
                                    lambda hh=hh, cs=cs: nc.tensor.matmul(PS[po[hh]][:, cs], S16[:, hh * 128:(hh + 1) * 128], QT[:, cs],
                                                                          start=False, stop=True)],
                                   ['VT', ('ATT', hh), 'S16', 'QT'], [psk(po[hh])])
                    pd = bank('sm')
                    b.pe_group([lambda pd=pd, c=c: nc.tensor.matmul(PS[pd][:, 0:256], KET[0:CH, c, :], VT[0:CH, c, :], start=True, stop=True)],
                               ['KET', 'VT'], [psk(pd)])
                    b.op('dve', lambda pd=pd: nc.vector.tensor_tensor(TMPS[:, :], PS[pd][:, 0:256], MASKS[:, :], ALU.mult), [psk(pd), 'MASKS'], ['TMPS'])
                    b.op('dve', lambda c=c: nc.vector.scalar_tensor_tensor(S32[:, :], S32[:, :], E1[:, (c + 1) * CH - 1:(c + 1) * CH], TMPS[:, :],
                                                                          ALU.mult, ALU.add), ['S32', 'E1', 'TMPS'], ['S32'])
                    if c < nchk - 1:
                        b.op('act', lambda: nc.scalar.copy(out=S16[:, :], in_=S32[:, :]), ['S32'], ['S16'])
                    if c == 0 and m + 1 < 8:
                        nxt = stage1(m + 1)
                for hh in range(2):
                    b.dma('sp', gla_out[l, 2 * m + hh], S32[hh * 64:(hh + 1) * 64, hh * 128:(hh + 1) * 128],
                          ['S32'], [('glad', id(gla_out), l, 2 * m + hh)], 'S32st')
                for hh in range(2):
                    sq = SQ[hh]
                    b.op('act', lambda hh=hh, sq=sq: nc.scalar.activation(out=sq[:, 0:T], in_=PS[po[hh]][:, 0:T], func=AF.Square),
                         [psk(po[hh])], [('SQ', hh)])
                    pn = bank('sm')
                    b.pe_group([lambda pn=pn, sq=sq: nc.tensor.matmul(PS[pn][:, 0:T], ONES16[:, :], sq[:, 0:T], start=True, stop=True)],
                               [('SQ', hh), 'ONES16'], [psk(pn)])
                    b.op('dve', lambda pn=pn: nc.vector.tensor_scalar(RS[:, 0:T], PS[pn][:, 0:T], 1.0 / 128, EPS, ALU.mult, ALU.add), [psk(pn)], ['RS'])
                    b.op('act', lambda: nc.scalar.activation(out=RS[:, 0:T], in_=RS[:, 0:T], func=AF.Sqrt), ['RS'], ['RS'])
                    b.op('dve', lambda: nc.vector.reciprocal(RS[:, 0:T], RS[:, 0:T]), ['RS'], ['RS'])
                    b.op('dve', lambda hh=hh: nc.vector.tensor_tensor(T1[hh][:, 0:T], PS[po[hh]][:, 0:T], RS[:, 0:T], ALU.mult),
                         [psk(po[hh]), 'RS'], [('T1', hh)])
                    b.op('dve', lambda hh=hh, m=m: nc.vector.scalar_tensor_tensor(O[:, 2 * m + hh, 0:T], T1[hh][:, 0:T], GON[:, l:l + 1], SG[:, hh, 0:T],
                                                                                ALU.mult, ALU.mult),
                         [('T1', hh), 'GON', ('SG', hh)], [('O', 2 * m + hh)])
            def gstage(gb):
                s = load_w([(w_in[l][:, U0 + gb * 128:U0 + (gb + 1) * 128], 0), (w_in[l][:, VB0 + gb * 128:VB0 + (gb + 1) * 128], 128)])
                pu = proj_fm(s, 0, 128, T, XN, ['XN'])
                pv = bank('mm')
                for tt in range(ntt):
                    proj_tm(s, 128, 128, tt, tp, pv, tt * 128)
                return pu, pv

            gnx = gstage(0)
            for gb in range(8):
                pu, pv = gnx
                b.op('act', lambda pu=pu: nc.scalar.activation(out=T1[0][:, 0:T], in_=PS[pu][:, 0:T], func=AF.Gelu), [psk(pu)], [('T1', 0)])
                b.op('act', lambda pv=pv: nc.scalar.activation(out=VBF[0:tp, 0:ntt, :].rearrange("p a d -> p (a d)"), in_=PS[pv][0:tp, 0:ntt * 128],
                                                              func=AF.Gelu), [psk(pv)], ['VBF'])
                b.op('dve', lambda: nc.vector.tensor_copy(KET[0:tp, 0:ntt, :].rearrange("p a d -> p (a d)"),
                                                          VBF[0:tp, 0:ntt, :].rearrange("p a d -> p (a d)")), ['VBF'], ['KET'])
                if vs_out is not None:
                    b.dma('sp', vs_out[l, :, gb * 128:(gb + 1) * 128], VBF[0:tp, 0, :], ['VBF'], [], 'VBF')
                if gb + 1 < 8:
                    gnx = gstage(gb + 1)
                pm = bank('mm')
                for tt in range(ntt):
                    b.pe_group([lambda tt=tt, pm=pm, gb=gb: nc.tensor.matmul(PS[pm][:, tt * 128:tt * 128 + tp], KET[0:tp, tt, :], WST[0:tp, l, gb, 0:tp],
                                                                           start=True, stop=False),
                                lambda tt=tt, pm=pm, gb=gb: nc.tensor.matmul(PS[pm][:, tt * 128:tt * 128 + tp], ONES16[0:33, :], BSP2[0:33, l, gb, 0:tp],
                                                                           start=False, stop=True)],
                               ['KET', 'WST', 'ONES16', 'BSP2'], [psk(pm)])
                b.op('dve', lambda pm=pm, gb=gb: nc.vector.tensor_tensor(O[:, 16 + gb, 0:T], PS[pm][:, 0:T], T1[0][:, 0:T], ALU.mult),
                     [psk(pm), ('T1', 0)], [('O', 16 + gb)])
            def lstage(cbk):
                s = load_w([(w_in[l][:, XC0 + cbk * 128:XC0 + (cbk + 1) * 128], 0), (w_in[l][:, GC0 + cbk * 128:GC0 + (cbk + 1) * 128], 128)])
                return proj_fm(s, 0, 128, T, XN, ['XN']), proj_fm(s, 128, 128, T, XN, ['XN'])

            lnx = lstage(0)
            for cbk in range(8):
                px, pc = lnx
                wg = WG[cbk % 2]
                kwg = ('WG', cbk % 2)
                b.dma('pool', wg[:, 0, :], w_r[l, cbk], [], [kwg], kwg)
                b.dma('pool', wg[:, 1, :], w_i[l, cbk], [], [kwg], kwg)
                if conv_in is None:
                    if first:
                        b.op('dve', lambda: nc.vector.memset(XC[:, 0:3], 0.0), [], ['XC'])
                    else:
                        b.op('dve', lambda cbk=cbk: nc.vector.tensor_copy(XC[:, 0:3], CS[:, sq_, l, cbk, :]), ['CS'], ['XC'])
                else:
                    for j3 in range(3):
                        b.dma('sp', XC[:, j3:j3 + 1], conv_in[l, j3, cbk * 128:(cbk + 1) * 128].rearrange("(p o) -> p o", o=1), [], ['XC'], 'XC')
                b.op('act', lambda px=px: nc.scalar.copy(out=XC[:, 3:3 + T], in_=PS[px][:, 0:T]), [psk(px)], ['XC'])
                b.op('dve', lambda cbk=cbk: nc.vector.tensor_copy(CS[:, sq_, l, cbk, :], XC[:, T:T + 3]), ['XC'], ['CS'])
                if last:
                    for j3 in range(3):
                        b.dma('sp', conv_out[l, j3, cbk * 128:(cbk + 1) * 128].rearrange("(p o) -> p o", o=1), XC[:, T + j3:T + j3 + 1], ['XC'], [], 'XCst')
                b.op('dve', lambda cbk=cbk: nc.vector.tensor_scalar(XCV[:, 0:T], XC[:, 0:T], CWT[:, l, 0, cbk:cbk + 1], CBT[:, l, cbk:cbk + 1],
                                                                   ALU.mult, ALU.add), ['XC', 'CWT', 'CBT'], ['XCV'])
                for j in range(1, 4):
                    b.op('dve', lambda cbk=cbk, j=j: nc.vector.scalar_tensor_tensor(XCV[:, 0:T], XC[:, j:j + T], CWT[:, l, j, cbk:cbk + 1], XCV[:, 0:T],
                                                                                   ALU.mult, ALU.add), ['XC', 'CWT', 'XCV'], ['XCV'])
                if cbk + 1 < 8:
                    lnx = lstage(cbk + 1)
                b.op('act', lambda: nc.scalar.copy(out=XCV16[:, 0:T], in_=XCV[:, 0:T]), ['XCV'], ['XCV16'])
                pr = bank('sm')
                b.pe_group([lambda pr=pr, wg=wg: nc.tensor.matmul(PS[pr][:, 0:T], wg[:, 0, :], XCV16[:, 0:T], start=True, stop=True)],
                           [kwg, 'XCV16'], [psk(pr)])
                pg = bank('sm')
                b.pe_group([lambda pg=pg, wg=wg: nc.tensor.matmul(PS[pg][:, 0:T], wg[:, 1, :], XCV16[:, 0:T], start=True, stop=True)],
                           [kwg, 'XCV16'], [psk(pg)])
                A_, Bt, Ig = EB[0], EB[1], LNB
                b.op('act', lambda pr=pr, cbk=cbk: nc.scalar.activation(out=A_[:, 0:T], in_=PS[pr][:, 0:T], func=AF.Sigmoid, bias=BRT[:, l, cbk:cbk + 1]),
                     [psk(pr), 'BRT'], [('EB', 0)])
                b.op('act', lambda pg=pg, cbk=cbk: nc.scalar.activation(out=Ig[:, 0:T], in_=PS[pg][:, 0:T], func=AF.Sigmoid, bias=BIT[:, l, cbk:cbk + 1]),
                     [psk(pg), 'BIT'], ['LNB'])
                b.op('act', lambda cbk=cbk: nc.scalar.activation(out=A_[:, 0:T], in_=A_[:, 0:T], func=AF.Exp, scale=LCT[:, l, cbk:cbk + 1]),
                     [('EB', 0), 'LCT'], [('EB', 0)])
                b.op('dve', lambda: nc.vector.tensor_tensor(Bt[:, 0:T], A_[:, 0:T], A_[:, 0:T], ALU.mult), [('EB', 0)], [('EB', 1)])
                b.op('dve', lambda: nc.vector.tensor_scalar(Bt[:, 0:T], Bt[:, 0:T], -1.0, 1.0, ALU.mult, ALU.add), [('EB', 1)], [('EB', 1)])
                b.op('act', lambda: nc.scalar.activation(out=Bt[:, 0:T], in_=Bt[:, 0:T], func=AF.Sqrt), [('EB', 1)], [('EB', 1)])
                b.op('dve', lambda: nc.vector.tensor_tensor(Ig[:, 0:T], Ig[:, 0:T], XCV[:, 0:T], ALU.mult), ['LNB', 'XCV'], ['LNB'])
                b.op('dve', lambda: nc.vector.tensor_tensor(Bt[:, 0:T], Bt[:, 0:T], Ig[:, 0:T], ALU.mult), [('EB', 1), 'LNB'], [('EB', 1)])
                if lru_in is not None:
                    b.dma('sp', HS[:, sq_, l, cbk:cbk + 1], lru_in[l, cbk * 128:(cbk + 1) * 128].rearrange("(p o) -> p o", o=1), [], ['HS'], 'HS')
                b.op('dve', lambda cbk=cbk: nc.vector.tensor_tensor_scan(BL[:, 0:T], A_[:, 0:T], Bt[:, 0:T], HS[:, sq_, l, cbk:cbk + 1], ALU.mult, ALU.add),
                     [('EB', 0), ('EB', 1), 'HS'], ['BL'])
                b.op('dve', lambda cbk=cbk: nc.vector.tensor_copy(HS[:, sq_, l, cbk:cbk + 1], BL[:, T - 1:T]), ['BL'], ['HS'])
                if last:
                    b.dma('sp', lru_out[l, cbk * 128:(cbk + 1) * 128].rearrange("(p o) -> p o", o=1), HS[:, sq_, l, cbk:cbk + 1], ['HS'], [], 'HSst')
                b.op('act', lambda pc=pc: nc.scalar.activation(out=T1[1][:, 0:T], in_=PS[pc][:, 0:T], func=AF.Gelu), [psk(pc)], [('T1', 1)])
                b.op('dve', lambda cbk=cbk: nc.vector.tensor_tensor(O[:, 24 + cbk, 0:T], BL[:, 0:T], T1[1][:, 0:T], ALU.mult),
                     ['BL', ('T1', 1)], [('O', 24 + cbk)])
            okeys = [('O', c) for c in range(NCH)]
            for oc2 in range(16):
                s = load_w([(w_out[l][:, oc2 * 256:(oc2 + 1) * 256], 0)])
                for cc in range(2):
                    pi = proj_fm(s, cc * 128, 128, T, O, okeys)
                    c = oc2 * 2 + cc
                    b.op('dve', lambda pi=pi, c=c: nc.vector.tensor_tensor(X[:, c, 0:T], X[:, c, 0:T], PS[pi][:, 0:T], ALU.add),
                         [psk(pi), ('X', c)], [('X', c)])
            rmsnorm(LN2T[:, l, :], 'LN2T', T)
            for blk in range(8):
                hb = O[:, (blk % 2) * 16:(blk % 2) * 16 + 16, :]
                hkeys = [('O', (blk % 2) * 16 + j) for j in range(16)]
                for j in range(8):
                    s = load_w([(w_up[l][:, blk * 2048 + j * 256:blk * 2048 + (j + 1) * 256], 0)])
                    for cc in range(2):
                        pi = proj_fm(s, cc * 128, 128, T, XN, ['XN'])
                        t1 = T1[cc]
                        b.op('act', lambda pi=pi, t1=t1: nc.scalar.activation(out=t1[:, 0:T], in_=PS[pi][:, 0:T], func=AF.Relu), [psk(pi)], [('T1', cc)])
                        b.op('dve', lambda t1=t1, j=j, cc=cc, hb=hb: nc.vector.tensor_tensor(hb[:, j * 2 + cc, 0:T], t1[:, 0:T], t1[:, 0:T], ALU.mult),
                             [('T1', cc)], [hkeys[j * 2 + cc]])
                for oc4 in range(8):
                    s = load_w([(w_down[l][blk * 2048:(blk + 1) * 2048, oc4 * 512:(oc4 + 1) * 512], 0)], nk=16, wide=True)
                    for cc in range(4):
                        pi = proj_fm(s, cc * 128, 128, T, hb, hkeys, nk=16, wide=True)
                        c = oc4 * 4 + cc
                        b.op('dve', lambda pi=pi, c=c: nc.vector.tensor_tensor(X[:, c, 0:T], X[:, c, 0:T], PS[pi][:, 0:T], ALU.add),
                             [psk(pi), ('X', c)], [('X', c)])

        for p in range(npass):
            wstate['cached'] = p > 0
            load_x(xp[p * TP:(p + 1) * TP, :], TP)
            for l in range(2):
                layer(l, TP, 0, p == 0, p == npass - 1, None if p == 0 else glap, glap, None, convp, None, lrup, None)
            store_y(yp[p * TP:(p + 1) * TP, :], TP)
        for sp_ in range(nsp):
            wstate['cached'] = npass > 0 or sp_ > 0
            load_x(xs[sp_], 32)
            for l in range(2):
                layer(l, 32, 1, True, True, sgla[sp_], glas[sp_], sconv[sp_], convs[sp_], slru[sp_], lrus[sp_], vs[sp_])
            store_y(ys[sp_], 32)
        b.finish()
    return nc


def _consts():
    t = np.arange(512)
    cmask = np.broadcast_to((t % 128 != 0).astype(np.float32)[None, :], (128, 512)).copy()
    j = np.arange(128)
    mu = (j[:, None] <= j[None, :]).astype(np.float32)
    masks = ((j[:, None] // 64) == (np.arange(256)[None, :] // 128)).astype(np.float32)
    rowmask = ((j[:, None] // 64) == np.arange(2)[None, :]).astype(np.float32)
    ident = np.eye(128, dtype=np.float32)
    return dict(cmask=cmask, mu=mu, masks=masks, rowmask=rowmask, ident=ident)


_PROG = {}

CORES = 8
PROMPT_AT = {0: 0, 4: 1}
NSP = 1


def run(inputs, npass, cores=None, prompt_at=None, nsp=None, trace=False):
    cores = CORES if cores is None else cores
    prompt_at = PROMPT_AT if prompt_at is None else prompt_at
    nsp = NSP if nsp is None else nsp
    key = (npass, nsp)
    if key not in _PROG:
        _PROG[key] = build_program(npass, nsp)
    nc = _PROG[key]
    f = lambda a: np.ascontiguousarray(np.asarray(a, dtype=np.float32))
    shared = {k: f(inputs[k]) for k in ('ln1', 'w_in', 'w_alpha2', 'b_alpha', 'g_onorm', 'w_spatial', 'b_spatial', 'conv_w', 'conv_b',
                                        'w_rgate', 'b_rgate', 'w_igate', 'b_igate', 'lru_lambda', 'w_out', 'ln2', 'w_up', 'w_down',
                                        'ln_final')}
    shared.update(_consts())
    xpf = f(inputs['x_prompt'])
    xsf = f(inputs['x_sample'])
    sg, scv, slr = f(inputs['state_gla']), f(inputs['state_conv']), f(inputs['state_lru'])
    zeros_p = np.zeros((npass * TP, D), np.float32)
    nsamp = min(xsf.shape[0], cores * nsp)
    in_maps = []
    for c in range(cores):
        m = dict(shared)
        m['xp'] = np.ascontiguousarray(xpf[prompt_at[c], :npass * TP]) if c in prompt_at else zeros_p
        ids = [min(c * nsp + j, xsf.shape[0] - 1) for j in range(nsp)]
        m['xs'] = np.ascontiguousarray(xsf[ids])
        m['sgla'] = np.ascontiguousarray(np.stack([sg[:, i] for i in ids]))
        m['sconv'] = np.ascontiguousarray(np.stack([scv[:, i] for i in ids]))
        m['slru'] = np.ascontiguousarray(np.stack([slr[:, i] for i in ids]))
        in_maps.append(m)
    res = run_bass_kernel_spmd(nc, in_maps, core_ids=list(range(cores)), **({'trace': True} if trace else {}))
    r = res.results
    pcs = sorted(prompt_at, key=lambda c: prompt_at[c])
    y_prompt = np.stack([r[c]['yp'] for c in pcs])
    gla_p = np.stack([r[c]['glap'] for c in pcs], axis=1)
    conv_p = np.stack([r[c]['convp'] for c in pcs], axis=1)
    lru_p = np.stack([r[c]['lrup'] for c in pcs], axis=1)
    samp = [(i // nsp, i % nsp) for i in range(nsamp)]
    y_sample = np.stack([r[c]['ys'][j] for c, j in samp])
    gla_s = np.stack([r[c]['glas'][j] for c, j in samp], axis=1)
    conv_s = np.stack([r[c]['convs'][j] for c, j in samp], axis=1)
    lru_s = np.stack([r[c]['lrus'][j] for c, j in samp], axis=1)
    v_s = np.stack([r[c]['vs'][j] for c, j in samp], axis=1)
    outs = (y_prompt, y_sample, gla_p, conv_p, lru_p, gla_s, conv_s, lru_s, v_s)
    return tuple(np.ascontiguousarray(o.astype(np.float32)) for o in outs), res


def kernel(**inputs):
    outs, _ = run(inputs, 4096 // TP)
    return outs
```
